# Optimizing a Trainium2 kernel written in Bass

```python
import jax, jax.numpy as jnp
from jax import lax
import numpy as np

D_MODEL = 1024
BATCH = 4
SEQ = 8192
DEPTH = 2

D_MIX = D_MODEL
N_MIXERS = 4
W_GROUP = D_MIX // N_MIXERS
HEAD_DIM = 64
N_HEADS = W_GROUP // HEAD_DIM
CONV_A_WIDTH = 31
POOL_WINDOWS = (2, 4, 8, 16)
POOL_GROUP = W_GROUP // len(POOL_WINDOWS)
CHUNK = 128
SHORT_CONV_WIDTH = 3
FFN_CONV_WIDTH = 3
D_FF = ((8 * D_MODEL // 3 + 127) // 128) * 128
N_MOD = 6
IN_A = 2 * W_GROUP
IN_B = W_GROUP
IN_C = 2 * W_GROUP
IN_D = 3 * W_GROUP
D_IN = IN_A + IN_B + IN_C + IN_D
EPS = 1e-6

kernel_name = "hybrid_conv_pool_sgu_shortconv_block"


def rms_norm(x, g):
    xf = x.astype(jnp.float32)
    y = xf * lax.rsqrt(jnp.mean(xf * xf, axis=-1, keepdims=True) + EPS)
    return (y * g.astype(jnp.float32)).astype(x.dtype)


def group_norm(x, n_groups, g, b):
    shp = x.shape
    xf = x.astype(jnp.float32).reshape(shp[:-1] + (n_groups, shp[-1] // n_groups))
    mu = jnp.mean(xf, axis=-1, keepdims=True)
    var = jnp.mean(jnp.square(xf - mu), axis=-1, keepdims=True)
    y = ((xf - mu) * lax.rsqrt(var + EPS)).reshape(shp)
    return (y * g.astype(jnp.float32) + b.astype(jnp.float32)).astype(x.dtype)


def causal_dwconv(x, w):
    k, ch = w.shape
    return lax.conv_general_dilated(
        x, w[:, None, :].astype(x.dtype), window_strides=(1,), padding=[(k - 1, 0)],
        dimension_numbers=("NWC", "WIO", "NWC"), feature_group_count=ch)


def conformer_conv(pa, conv_w, conv_b, gn_g, gn_b):
    a, g = jnp.split(pa, 2, axis=-1)
    h = a * jax.nn.sigmoid(g)
    h = causal_dwconv(h, conv_w) + conv_b
    h = group_norm(h, N_HEADS, gn_g, gn_b)
    return jax.nn.silu(h)


def pool_mixer(pb, pool_w, pool_scale):
    bn, s, ch = pb.shape
    xf = pb.astype(jnp.float32)
    csp = jnp.concatenate([jnp.zeros((bn, 1, ch), jnp.float32), jnp.cumsum(xf, axis=1)], axis=1)
    upper = csp[:, 1:]
    pos = jnp.arange(1, s + 1, dtype=jnp.float32)[None, :, None]
    outs = []
    for gi, w in enumerate(POOL_WINDOWS):
        sl = slice(gi * POOL_GROUP, (gi + 1) * POOL_GROUP)
        lower = jnp.concatenate([jnp.zeros((bn, w - 1, POOL_GROUP), jnp.float32), csp[:, :s + 1 - w, sl]], axis=1)
        mean = (upper[..., sl] - lower) / jnp.minimum(pos, w)
        outs.append(mean - xf[..., sl])
    y = jnp.stack(outs, axis=2).astype(pb.dtype)
    y = jnp.einsum("bsgc,gcd->bsgd", y, pool_w).reshape(bn, s, ch)
    return y * pool_scale


def spatial_gating(pc, ln_g, ln_b, w_s, b_s):
    u, v = jnp.split(pc, 2, axis=-1)
    v = group_norm(v, 1, ln_g, ln_b)
    bn, s, ch = v.shape
    v = v.reshape(bn, s // CHUNK, CHUNK, N_HEADS, HEAD_DIM)
    mask = jnp.tril(jnp.ones((CHUNK, CHUNK), dtype=bool))
    w = jnp.where(mask[None], w_s, jnp.zeros_like(w_s))
    sv = jnp.einsum("hts,bnshc->bnthc", w, v) + jnp.transpose(b_s)[None, None, :, :, None]
    return u * sv.reshape(bn, s, ch)


def short_conv(pd, conv_w):
    bg, cg, h = jnp.split(pd, 3, axis=-1)
    return bg * causal_dwconv(cg * h, conv_w)


def setup_inputs(seed: int = 0) -> dict:
    key = jax.random.key(seed)
    ks = jax.random.split(key, 24)
    f32 = jnp.float32
    L = DEPTH

    def nrm(k, shape, scale):
        return scale * jax.random.normal(k, shape, f32)

    return {
        "x": nrm(ks[0], (BATCH, SEQ, D_MODEL), 1.0),
        "c": nrm(ks[1], (BATCH, D_MODEL), 1.0),
        "norm1_g": 1.0 + nrm(ks[2], (L, D_MODEL), 0.05),
        "ada_w": nrm(ks[3], (L, D_MODEL, N_MOD * D_MODEL), 0.5 * D_MODEL ** -0.5),
        "ada_b": nrm(ks[4], (L, N_MOD * D_MODEL), 0.02),
        "w_in": nrm(ks[5], (L, D_MODEL, D_IN), D_MODEL ** -0.5),
        "conv_a_w": nrm(ks[6], (L, CONV_A_WIDTH, W_GROUP), CONV_A_WIDTH ** -0.5),
        "conv_a_b": nrm(ks[7], (L, W_GROUP), 0.02),
        "gn_a_g": 1.0 + nrm(ks[8], (L, W_GROUP), 0.05),
        "gn_a_b": nrm(ks[9], (L, W_GROUP), 0.02),
        "pool_w": nrm(ks[10], (L, len(POOL_WINDOWS), POOL_GROUP, POOL_GROUP), POOL_GROUP ** -0.5),
        "pool_scale": 1.0 + nrm(ks[11], (L, W_GROUP), 0.1),
        "sgu_ln_g": 1.0 + nrm(ks[12], (L, W_GROUP), 0.05),
        "sgu_ln_b": nrm(ks[13], (L, W_GROUP), 0.02),
        "sgu_w": nrm(ks[14], (L, N_HEADS, CHUNK, CHUNK), CHUNK ** -0.5),
        "sgu_b": 1.0 + nrm(ks[15], (L, N_HEADS, CHUNK), 0.1),
        "conv_d_w": nrm(ks[16], (L, SHORT_CONV_WIDTH, W_GROUP), SHORT_CONV_WIDTH ** -0.5),
        "w_out": nrm(ks[17], (L, D_MIX, D_MODEL), D_MIX ** -0.5),
        "norm2_g": 1.0 + nrm(ks[18], (L, D_MODEL), 0.05),
        "ffn_w_gate": nrm(ks[19], (L, D_MODEL, D_FF), D_MODEL ** -0.5),
        "ffn_w_up": nrm(ks[20], (L, D_MODEL, D_FF), D_MODEL ** -0.5),
        "ffn_conv_w": nrm(ks[21], (L, FFN_CONV_WIDTH, D_FF), FFN_CONV_WIDTH ** -0.5),
        "ffn_w_down": nrm(ks[22], (L, D_FF, D_MODEL), D_FF ** -0.5),
        "final_g": 1.0 + nrm(ks[23], (D_MODEL,), 0.05),
    }


def reference(x, c, norm1_g, ada_w, ada_b, w_in, conv_a_w, conv_a_b, gn_a_g, gn_a_b,
              pool_w, pool_scale, sgu_ln_g, sgu_ln_b, sgu_w, sgu_b, conv_d_w, w_out,
              norm2_g, ffn_w_gate, ffn_w_up, ffn_conv_w, ffn_w_down, final_g):
    c_act = jax.nn.silu(c)
    for l in range(DEPTH):
        mod = (c_act @ ada_w[l] + ada_b[l])[:, None, :]
        sh1, sc1, g1, sh2, sc2, g2 = jnp.split(mod, N_MOD, axis=-1)

        h = rms_norm(x, norm1_g[l]) * (1 + sc1) + sh1
        p = h @ w_in[l]
        pa, pb, pc, pd = jnp.split(p, [IN_A, IN_A + IN_B, IN_A + IN_B + IN_C], axis=-1)
        ya = conformer_conv(pa, conv_a_w[l], conv_a_b[l], gn_a_g[l], gn_a_b[l])
        yb = pool_mixer(pb, pool_w[l], pool_scale[l])
        yc = spatial_gating(pc, sgu_ln_g[l], sgu_ln_b[l], sgu_w[l], sgu_b[l])
        yd = short_conv(pd, conv_d_w[l])
        y = jnp.concatenate([ya, yb, yc, yd], axis=-1) @ w_out[l]
        x = x + g1 * y

        h = rms_norm(x, norm2_g[l]) * (1 + sc2) + sh2
        a = jax.nn.silu(causal_dwconv(h @ ffn_w_gate[l], ffn_conv_w[l]))
        f = (a * (h @ ffn_w_up[l])) @ ffn_w_down[l]
        x = x + g2 * f
    return rms_norm(x, final_g)
```

```python
import numpy as np
from contextlib import ExitStack
import concourse.bass as bass
import concourse.mybir as mybir
from concourse.bass_utils import run_bass_kernel_spmd

F32 = mybir.dt.float32
BF16 = mybir.dt.bfloat16
AF = mybir.ActivationFunctionType
ALU = mybir.AluOpType
AX = mybir.AxisListType
EPS = 1e-6
HALO = 128
NS = 3
SLOT = 5632
DEBUG_LABELS = False
ADA_POS = [("mix", 0), ("mix", 1), ("mix", 2), ("mix", 3), ("wout", 0), ("wout", 1),
           ("ffn", 1), ("ffn", 3), ("ffn", 5), ("ffn", 7), ("ffn", 9), ("down", 0)]
ADEPTH = 4
NROT = 6


class Prog:
    def __init__(self, nc, stack):
        self.nc = nc
        self.stack = stack
        self.ops = []
        self.lastw = {}
        self.readers = {}
        self.engs = {"pe": nc.tensor, "act": nc.scalar, "dve": nc.vector, "pool": nc.gpsimd, "sp": nc.sync}
        self.esem = {e: stack.enter_context(nc.semaphore("s_" + e)) for e in self.engs}
        self.dsems = {}

    def dma_sem(self, name):
        if name not in self.dsems:
            self.dsems[name] = self.stack.enter_context(self.nc.semaphore("d_" + name))
        return name

    def add(self, eng, fn, reads=(), writes=(), dma=None):
        idx = len(self.ops)
        deps = {}
        for k in reads:
            p = self.lastw.get(k)
            if p is not None:
                deps[p] = "raw"
        for k in writes:
            p = self.lastw.get(k)
            if p is not None and p not in deps:
                deps[p] = "waw"
            for r in self.readers.get(k, ()):
                if r not in deps:
                    deps[r] = "war"
        self.ops.append(dict(eng=eng, fn=fn, deps=deps, dma=dma, sig=False, ph=getattr(self, "phase", "")))
        for k in reads:
            self.readers.setdefault(k, []).append(idx)
        for k in writes:
            self.lastw[k] = idx
            self.readers[k] = []
        return idx

    @staticmethod
    def _needed(c, p, kind):
        if p["dma"] is not None:
            return True
        if c["dma"] is None and c["eng"] == p["eng"]:
            if c["eng"] == "pe":
                return False
            return kind == "raw"
        return True

    def emit(self):
        ops = self.ops
        for c in ops:
            keep = {}
            for p_i, kind in c["deps"].items():
                p = ops[p_i]
                if not self._needed(c, p, kind):
                    continue
                key = ("d", p["dma"]) if p["dma"] is not None else ("e", p["eng"])
                if key not in keep or keep[key] < p_i:
                    keep[key] = p_i
            c["wdeps"] = list(keep.values())
            c["deps"] = None
            for p_i in c["wdeps"]:
                ops[p_i]["sig"] = True
        ecount = {e: 0 for e in self.engs}
        dcount = {d: 0 for d in self.dsems}
        for o in ops:
            if o["dma"] is not None:
                dcount[o["dma"]] += 1
                o["val"] = 16 * dcount[o["dma"]]
            elif o["sig"]:
                ecount[o["eng"]] += 1
                o["val"] = ecount[o["eng"]]
        waited = {e: {} for e in self.engs}
        nwait = 0
        for o in ops:
            eng = self.engs[o["eng"]]
            w = waited[o["eng"]]
            for p_i in o["wdeps"]:
                p = ops[p_i]
                if p["dma"] is not None:
                    sname, sem = ("d", p["dma"]), self.dsems[p["dma"]]
                else:
                    sname, sem = ("e", p["eng"]), self.esem[p["eng"]]
                if w.get(sname, 0) < p["val"]:
                    eng.wait_ge(sem, p["val"])
                    w[sname] = p["val"]
                    nwait += 1
            if o["fn"] is not None:
                inst = o["fn"](eng)
                try:
                    o["iname"] = inst.ins.name
                except Exception:
                    o["iname"] = None
                if o["dma"] is not None:
                    inst.then_inc(self.dsems[o["dma"]], 16)
                elif o["sig"]:
                    inst.then_inc(self.esem[o["eng"]], 1)
        return dict(nops=len(ops), nwait=nwait, ecount=ecount, dcount=dcount)


def prm_layout(L):
    off = {}
    n = 0

    def add(name, w):
        nonlocal n
        off[name] = n
        n += w

    for l in range(L):
        for nm, w in (("n1g", 8), ("n2g", 8), ("cab", 2), ("gng", 2), ("gnb", 2), ("psc", 2), ("lng", 2), ("lnb", 2),
                      ("caw", 62), ("cdw", 6), ("fcw", 66), ("adab", 48)):
            add(f"{l}.{nm}", w)
    add("fg", 8)
    add("c", 8)
    add("mask", 1)
    add("pos", 16)
    return off, n


def vec_cols(v):
    return np.ascontiguousarray(v.reshape(-1, 128).T)


def make_tiles(ntok, tw):
    tiles = []
    t = 0
    while t < HALO + ntok:
        w = min(tw + (HALO if t == 0 else 0), HALO + ntok - t)
        tiles.append((t, w))
        t += w
    return tiles


class Kern:
    def __init__(s, nc, st, L, ntok, TW, final=True):
        s.nc, s.st, s.L, s.ntok, s.final = nc, st, L, ntok, final
        s.NT = HALO + ntok
        s.tiles = make_tiles(ntok, TW)
        s.TW = max(w for _, w in s.tiles)
        s.nrot = NROT
        s.P = Prog(nc, st)
        s.off, s.NPRM = prm_layout(L)
        s.nb = 0
        s.tick = 0
        s.seq = 0
        s.dq = []
        s.rot = {}
        s.ssc = {}
        s.sgu_done = set()
        s.nsg = 2
        s.pend = {}
        dt = nc.dram_tensor
        s.dX = dt("xs", [s.NT, 1024], F32, kind="ExternalInput").ap()
        s.dPrm = dt("prm", [128, s.NPRM], F32, kind="ExternalInput").ap()
        s.dTabs = dt("tabs", [128, L * 1024], F32, kind="ExternalInput").ap()
        s.dAdaW = dt("ada_w", [L, 1024, 6144], F32, kind="ExternalInput").ap()
        s.dWin = dt("w_in", [L, 1024, 2048], F32, kind="ExternalInput").ap()
        s.dWout = dt("w_out", [L, 1024, 1024], F32, kind="ExternalInput").ap()
        s.dWg = dt("ffn_w_gate", [L, 1024, 2816], F32, kind="ExternalInput").ap()
        s.dWu = dt("ffn_w_up", [L, 1024, 2816], F32, kind="ExternalInput").ap()
        s.dWd = dt("ffn_w_down", [L, 2816, 1024], F32, kind="ExternalInput").ap()
        s.dOut = dt("out", [ntok, 1024], F32, kind="ExternalOutput").ap()
        s.alloc()

    def sb(s, name, shape, dtype):
        return s.st.enter_context(s.nc.sbuf_tensor("sb_" + name, shape, dtype))

    def alloc(s):
        nc, L, TW = s.nc, s.L, s.TW
        s.prm = s.sb("prm", [128, s.NPRM], F32)
        s.xT = s.sb("xT", [128, 8, TW], F32)
        s.hb = s.sb("hb", [128, 8, TW], BF16)
        s.ws = [s.sb(f"ws{i}", [128, SLOT], BF16) for i in range(NS)]
        s.sqb = s.sb("sqb", [128, 2, 512], BF16)
        s.ident = s.sb("ident", [128, 128], F32)
        s.ones_f = s.sb("ones_f", [128, 128], F32)
        s.blk = s.sb("blk", [128, 128], F32)
        s.imb = s.sb("imb", [128, 128], F32)
        s.bcn = s.sb("bcn", [128, L, 2], F32)
        s.ones_b = s.sb("ones_b", [128, 128], BF16)
        s.epsc = s.sb("epsc", [128, 1], F32)
        s.modT = s.sb("modT", [128, L, 48], F32)
        s.gm = s.sb("gm", [128, L, 2, 8], F32)
        s.WTm = s.sb("WTm", [128, L, 4, 128], BF16)
        s.PW = s.sb("PW", [128, L, 2, 128], BF16)
        s.Bfull = s.sb("Bfull", [128, L, 2, 128], F32)
        s.cact = s.sb("cact", [128, 8], F32)
        s.cact_b = s.sb("cact_b", [128, 8], BF16)
        s.blk_b = s.sb("blk_b", [128, 128], BF16)
        s.wcol = s.sb("wcol", [128, 2], F32)
        s.invw = s.sb("invw", [128, 2], F32)
        s.rdiv16 = s.sb("rdiv16", [128, 2, 16], F32)
        s.tmp16 = s.sb("tmp16", [128, 16], F32)
        s.cglu = s.sb("cglu", [128, L, 2, 30], BF16)
        s.cpb = s.sb("cpb", [128, L, 2, 16], F32)
        s.cm = s.sb("cm", [128, L, 2, 2], BF16)
        s.cG = s.sb("cG", [128, L, 22, 2], F32)
        s.fz = s.sb("fz", [128, 4], F32)
        s.vst = s.sb("vst", [128, 6, 4], F32)
        s.ps = [s.st.enter_context(nc.psum_tensor(f"ps{b}", [128, 512], F32)) for b in range(8)]
        sizes = {}

        class UA:
            def __init__(u):
                u.o = 0
                u.items = []

            def get(u, name, shape, dtype, parts=128):
                esz = 4 if dtype == F32 else 2
                n = int(np.prod(shape[1:])) * esz
                n = (n + 3) // 4 * 4
                u.items.append((name, u.o, shape, dtype))
                u.o += n

        S_, M_, F_, N_ = UA(), UA(), UA(), UA()
        N_.get("stage", [128, 2, 1024], F32)
        N_.get("rstd", [128, TW], F32)
        N_.get("ntmp", [128, 4, 512], F32)
        S_.get("tabs", [128, L, 1024], F32)
        M_.get("y", [128, 8, TW], BF16)
        M_.get("glu", [128, 2, 32 + TW], BF16)
        M_.get("diagA", [128, 31, 128], BF16)
        M_.get("diagD", [128, 6, 128], BF16)
        for nm in ("sig", "hhs"):
            M_.get(nm, [128, 2, 512], F32)
        M_.get("cA", [128, ADEPTH, 512], F32)
        M_.get("sqA", [128, ADEPTH, 512], BF16)
        M_.get("rsA", [128, 2, 512], F32)
        M_.get("pbx", [128, 2, 16 + TW], F32)
        M_.get("sA", [128, 528], F32)
        M_.get("sB", [128, 528], F32)
        M_.get("pooled", [128, 4, 512], BF16)
        M_.get("vsb", [128, 4, 256], F32)
        M_.get("vtmp", [128, 4, 256], F32)
        M_.get("vn", [128, 2, 4, 256], BF16)
        M_.get("svb", [128, 2, TW], F32)
        M_.get("bgs", [128, 2, TW], F32)
        M_.get("m", [128, 2, 2 + TW], BF16)
        F_.get("a", [128, 22, TW], BF16)
        F_.get("G", [128, 4, 2 + TW], F32)
        for nm in ("G2", "ft", "fs"):
            F_.get(nm, [128, 4, 512], F32)
        usz = max(S_.o, M_.o, F_.o, N_.o)
        s.U = s.sb("U", [128, usz // 2], BF16)
        for ua in (S_, M_, F_, N_):
            for (name, o, shape, dtype) in ua.items:
                esz = 4 if dtype == F32 else 2
                n = int(np.prod(shape[1:]))
                ap = s.U[0:shape[0], o // 2: o // 2 + n * esz // 2]
                if dtype == F32:
                    ap = ap.bitcast(F32)
                if len(shape) == 3:
                    ap = ap.rearrange("p (a b) -> p a b", a=shape[1])
                elif len(shape) == 4:
                    ap = ap.rearrange("p (a b c) -> p a b c", a=shape[1], b=shape[2])
                setattr(s, name, ap)
        print("sbuf bytes remaining per partition:", nc.sbuf_bytes_remaining)

    def op(s, eng, fn, r=(), w=(), u=False):
        r = list(r)
        if u:
            r.append("U")
        idx = s.P.add(eng, fn, reads=r, writes=list(w))
        if DEBUG_LABELS:
            import sys as _sys
            f = _sys._getframe(1)
            if f.f_code.co_name == "mmg":
                f = f.f_back
            s.P.ops[idx]["lab"] = f"{f.f_code.co_name}:{f.f_lineno} w={list(w)[:1]}"

    def fence(s):
        s.P.add("dve", lambda e: e.memset(s.fz[:, 0:1], 0.0), writes=["U"])

    def bank(s):
        b = s.nb % s.nrot
        s.nb += 1
        return b

    def ri(s, name, n=2):
        v = s.rot.get(name, 0)
        s.rot[name] = v + 1
        return v % n

    def ri_wait(s, name, n=2):
        i = s.ri(name, n)
        while s.pend.get((name, i)):
            s.tk()
        s.pend[(name, i)] = True
        return i

    def pc(s, name, j=0, n=1):
        o = s.off[name] + j
        return s.prm[:, o:o + n]

    def defer(s, d, fn):
        s.dq.append((s.tick + d, s.seq, fn))
        s.seq += 1

    def run_due(s):
        while True:
            due = [x for x in s.dq if x[0] <= s.tick]
            if not due:
                return
            x = min(due, key=lambda z: (z[0], z[1]))
            s.dq.remove(x)
            x[2]()

    def tk(s, n=1):
        for _ in range(n):
            s.tick += 1
            s.run_due()

    def flush(s):
        while s.dq:
            s.tick += 1
            s.run_due()

    def mmg(s, out_ap, b, pairs, rkeys, u=False):
        n = len(pairs)
        for i, (l, r) in enumerate(pairs):
            s.op("pe", lambda e, l=l, r=r, i=i: e.matmul(out_ap, lhsT=l, rhs=r, start=(i == 0), stop=(i == n - 1)),
                 r=rkeys, w=[("ps", b)], u=u)

    def mmi(s, groups, u=False):
        n = max(len(g[2]) for g in groups)
        for k in range(n):
            for (out_ap, b, pairs) in groups:
                if k >= len(pairs):
                    continue
                l, r, keys = pairs[k]
                last = len(pairs) - 1
                s.op("pe", lambda e, out_ap=out_ap, l=l, r=r, k=k, last=last: e.matmul(out_ap, lhsT=l, rhs=r, start=(k == 0), stop=(k == last)),
                     r=keys, w=[("ps", b)], u=u)

    def plan_loads(s):
        s.loads = []
        adal = lambda l_: [[((8, 512, 0, 512), s.dAdaW[l_].rearrange("(kc p) n -> p kc n", p=128)[:, :, g_ * 512:(g_ + 1) * 512])] for g_ in range(12)]
        s.loads += adal(0)
        for ti in range(len(s.tiles)):
            for l in range(s.L):
                bg_ada = (ti == 0 and l + 1 < s.L)

                def bga(pos):
                    if bg_ada and pos in ADA_POS:
                        s.loads.append(adal(l + 1)[ADA_POS.index(pos)])
                win = s.dWin[l].rearrange("(kc p) n -> p kc n", p=128)
                for k_, c0 in enumerate((1024, 0, 512, 1536)):
                    s.loads.append([((8, 512, 0, 512), win[:, :, c0:c0 + 512])])
                    if k_ == 1:
                        bga(("mix", 0))
                    if k_ >= 1:
                        bga(("mix", k_))
                wo = s.dWout[l].rearrange("(kc p) n -> p kc n", p=128)
                for g in range(2):
                    s.loads.append([((8, 512, 0, 512), wo[:, :, g * 512:(g + 1) * 512])])
                    bga(("wout", g))
                wg = s.dWg[l].rearrange("(kc p) n -> p kc n", p=128)
                wu = s.dWu[l].rearrange("(kc p) n -> p kc n", p=128)
                for g in range(11):
                    s.loads.append([((8, 512, 0, 256), wg[:, :, g * 256:(g + 1) * 256]),
                                    ((8, 512, 256, 512), wu[:, :, g * 256:(g + 1) * 256])])
                    bga(("ffn", g))
                wd = s.dWd[l].rearrange("(kc p) n -> p kc n", p=128)
                for g in range(4):
                    s.loads.append([((22, 256, 0, 256), wd[:, :, g * 256:(g + 1) * 256])])
                    bga(("down", g))
        s.nissued = 0
        s.nused = 0
        for i in range(NS):
            s.P.dma_sem(f"w{i}")

    def issue_loads(s, upto):
        while s.nissued < min(upto, len(s.loads)):
            i = s.nissued
            slot = i % NS
            for part, ((kc, n, a, b), src) in enumerate(s.loads[i]):
                dst = s.ws[slot][:, 0:kc * n].rearrange("p (k n) -> p k n", k=kc)[:, :, a:b]
                wk = [("w", slot, 0), ("w", slot, 1)] if len(s.loads[i]) == 1 else [("w", slot, part)]
                s.P.add("pool", lambda e, dst=dst, src=src: e.dma_start(out=dst, in_=src), writes=wk, dma=f"w{slot}")
            s.nissued += 1

    def use_load(s, hold=0):
        i = s.nused
        s.nused += 1
        s.issue_loads(i + NS - hold)
        slot = i % NS
        return s.ws[slot], [("w", slot, 0), ("w", slot, 1)]

    def setup(s):
        P, L = s.P, s.L
        P.dma_sem("prm")
        P.add("sp", lambda e: e.dma_start(out=s.prm[:], in_=s.dPrm), writes=["prm"], dma="prm")
        P.dma_sem("tabs")
        P.add("sp", lambda e: e.dma_start(out=s.tabs, in_=s.dTabs.rearrange("p (l n) -> p l n", l=L)), reads=["U"], writes=["tabs"], dma="tabs")
        s.op("pool", lambda e: e.memset(s.ones_f[:], 1.0), w=["ones_f"])
        s.op("pool", lambda e: e.affine_select(out=s.ident[:], in_=s.ones_f[:], pattern=[[1, 128]], compare_op=ALU.is_equal,
                                               fill=0.0, base=0, channel_multiplier=-1), r=["ones_f"], w=["ident"])
        s.op("dve", lambda e: e.tensor_copy(out=s.ones_b[:], in_=s.ones_f[:]), r=["ones_f"], w=["ones_b"])
        s.op("dve", lambda e: e.memset(s.blk[:], 0.0), w=["blk"])
        s.op("dve", lambda e: e.memset(s.blk[0:64, 0:64], 1.0 / 64), w=["blk"])
        s.op("dve", lambda e: e.memset(s.blk[64:128, 64:128], 1.0 / 64), w=["blk"])
        s.op("dve", lambda e: e.memset(s.epsc[:], EPS), w=["epsc"])
        s.op("dve", lambda e: e.tensor_tensor(out=s.imb[:], in0=s.ident[:], in1=s.blk[:], op=ALU.subtract), r=["ident", "blk"], w=["imb"])
        for nm in ("cglu", "cpb", "cm", "cG"):
            t = getattr(s, nm)
            s.op("pool", lambda e, t=t: e.memset(t[:], 0.0), w=[nm] if nm != "cG" else [("cG", j) for j in range(22)])
        for j, (wl, wh) in enumerate(((2.0, 4.0), (8.0, 16.0))):
            s.op("dve", lambda e, j=j, wl=wl: e.memset(s.wcol[0:64, j:j + 1], wl), w=["wcol"])
            s.op("dve", lambda e, j=j, wh=wh: e.memset(s.wcol[64:128, j:j + 1], wh), w=["wcol"])
        s.op("dve", lambda e: e.reciprocal(out=s.invw[:], in_=s.wcol[:]), r=["wcol"], w=["invw"])
        for j in range(2):
            s.op("dve", lambda e, j=j: e.tensor_scalar(out=s.rdiv16[:, j, :], in0=s.pc("pos", 0, 16), scalar1=s.wcol[:, j:j + 1],
                                                       scalar2=None, op0=ALU.min), r=["prm", "wcol"], w=[("rdiv", j)])
            s.op("dve", lambda e, j=j: e.reciprocal(out=s.rdiv16[:, j, :], in_=s.rdiv16[:, j, :]), r=[("rdiv", j)], w=[("rdiv", j)])
        s.op("act", lambda e: e.activation(out=s.cact[:], in_=s.pc("c", 0, 8), func=AF.Silu), r=["prm"], w=["cact"])
        s.op("dve", lambda e: e.tensor_copy(out=s.cact_b[:], in_=s.cact[:]), r=["cact"], w=["cact_b"])
        s.op("dve", lambda e: e.tensor_copy(out=s.blk_b[:], in_=s.blk[:]), r=["blk"], w=["blk_b"])
        for l in range(L):
            b = s.bank()
            for j in range(2):
                s.op("pe", lambda e, b=b, j=j, l=l: e.matmul(s.ps[b][:, j:j + 1], lhsT=s.imb[:], rhs=s.pc(f"{l}.cab", j), start=True, stop=True),
                     r=["imb", "prm"], w=[("ps", b)])
            s.op("act", lambda e, b=b, l=l: e.activation(out=s.bcn[:, l, :], in_=s.ps[b][:, 0:2], func=AF.Copy), r=[("ps", b)], w=[("bcn", l)])
            s.op("pool", lambda e, l=l: e.affine_select(out=s.WTm[:, l], in_=s.tabs[:, l, 512:1024].rearrange("p (h t) -> p h t", h=4),
                                                        pattern=[[0, 4], [1, 128]], compare_op=ALU.is_ge, fill=0.0, base=0,
                                                        channel_multiplier=-1), r=["tabs"], w=[("WTm", l)], u=True)
            s.op("dve", lambda e, l=l: e.tensor_copy(out=s.PW[:, l], in_=s.tabs[:, l, 256:512].rearrange("p (j d) -> p j d", j=2)),
                 r=["tabs"], w=[("PW", l)], u=True)
            b = s.bank()
            for h in range(4):
                s.op("pe", lambda e, b=b, h=h, l=l: e.matmul(s.ps[b][:, h * 128:(h + 1) * 128], lhsT=s.ones_b[:], rhs=s.WTm[:, l, h, :],
                                                             start=True, stop=True), r=["ones_b", ("WTm", l)], w=[("ps", b)])
            for j in range(2):
                for hh in range(2):
                    rows = slice(64 * hh, 64 * hh + 64)
                    h = 2 * j + hh
                    s.op("dve", lambda e, b=b, l=l, j=j, h=h, rows=rows: e.scalar_tensor_tensor(
                        out=s.Bfull[rows, l, j, :], in0=s.ps[b][rows, h * 128:(h + 1) * 128],
                        scalar=s.prm[rows, s.off[f"{l}.lnb"] + j: s.off[f"{l}.lnb"] + j + 1],
                        in1=s.tabs[rows, l, j * 128:(j + 1) * 128], op0=ALU.mult, op1=ALU.add),
                        r=[("ps", b), "prm", "tabs"], w=[("Bfull", l, j, hh)], u=True)

    def maybe_ada(s, ti, l, pos):
        if ti == 0 and l + 1 < s.L and pos in ADA_POS:
            s.ada_group(l + 1, ADA_POS.index(pos))

    def ada_group(s, l, g):
        ws, wk = s.use_load()
        wv = ws[:, 0:4096].rearrange("p (k n) -> p k n", k=8)
        b2 = s.bank()
        for q in range(4):
            s.mmg(s.ps[b2][:, q:q + 1], b2, [(wv[:, kc, q * 128:(q + 1) * 128], s.cact_b[:, kc:kc + 1]) for kc in range(8)], wk + ["cact_b"])
        oB = s.off[f"{l}.adab"] + 4 * g
        s.op("dve", lambda e: e.tensor_tensor(out=s.modT[:, l, 4 * g:4 * g + 4], in0=s.ps[b2][:, 0:4], in1=s.prm[:, oB:oB + 4], op=ALU.add),
             r=[("ps", b2), "prm"], w=[("modTg", l, g)])
        if g == 11:
            s.P.add("dve", lambda e: e.tensor_copy(out=s.fz[:, 1:2], in_=s.fz[:, 2:3]), reads=[("modTg", l, q) for q in range(12)], writes=[("modT", l)])
            for wh, (gn, sc0) in enumerate((("n1g", 8), ("n2g", 32))):
                s.op("dve", lambda e, wh=wh, gn=gn, sc0=sc0: e.scalar_tensor_tensor(
                    out=s.gm[:, l, wh, :], in0=s.modT[:, l, sc0:sc0 + 8], scalar=1.0, in1=s.pc(f"{l}.{gn}", 0, 8),
                    op0=ALU.add, op1=ALU.mult), r=[("modT", l), "prm"], w=[("gm", l, wh)])
        s.tk()

    def segs(s, ti):
        W = s.tiles[ti][1]
        n = (W + 511) // 512
        step = ((W // 128 + n - 1) // n) * 128
        out = []
        c = 0
        while c < W:
            out.append((c, min(c + step, W)))
            c += step
        return out

    @staticmethod
    def seg_of(sg, col):
        for i, (c0, c1) in enumerate(sg):
            if c0 <= col < c1:
                return i
        raise ValueError(col)

    def ssb(s, si):
        return 8 - s.nsg + si

    def sumsq(s, kc, si, c0, c1):
        i = s.ri("sqb")
        w = c1 - c0
        cnt = s.ssc.get(si, 0)
        s.ssc[si] = (cnt + 1) % 8
        s.op("act", lambda e: e.activation(out=s.sqb[:, i, 0:w], in_=s.xT[:, kc, c0:c1], func=AF.Square),
             r=[("xT", kc, si)], w=[("sqb", i)])
        bs = s.ssb(si)
        s.op("pe", lambda e: e.matmul(s.ps[bs][:, 0:w], lhsT=s.ones_b[:], rhs=s.sqb[:, i, 0:w], start=(cnt == 0), stop=(cnt == 7)),
             r=[("sqb", i), "ones_b"], w=[("ps", bs)])

    def load_x(s, ti):
        tok0, W = s.tiles[ti]
        sg = s.segs(ti)
        for stt in range(W // 128):
            i = s.ri("stage")
            s.P.add("sp", lambda e, i=i, stt=stt: e.dma_start(out=s.stage[:, i, :], in_=s.dX[tok0 + stt * 128: tok0 + (stt + 1) * 128, :]),
                    reads=["U"], writes=[("stage", i)], dma=f"st{i}")
            si = s.seg_of(sg, stt * 128)
            for half in range(2):
                b = s.bank()
                for q in range(4):
                    kc = half * 4 + q
                    s.op("pe", lambda e, b=b, q=q, kc=kc, i=i: e.transpose(out=s.ps[b][:, q * 128:(q + 1) * 128],
                                                                          in_=s.stage[:, i, kc * 128:(kc + 1) * 128], identity=s.ident[:]),
                         r=[("stage", i), "ident"], w=[("ps", b)], u=True)
                eng = "act" if half == 0 else "dve"
                src = s.ps[b][:].rearrange("p (a t) -> p a t", a=4)
                dst = s.xT[:, half * 4:half * 4 + 4, stt * 128:(stt + 1) * 128]
                wk = [("xT", half * 4 + q, si) for q in range(4)]
                if eng == "act":
                    s.op("act", lambda e, src=src, dst=dst: e.activation(out=dst, in_=src, func=AF.Copy), r=[("ps", b)], w=wk)
                else:
                    s.op("dve", lambda e, src=src, dst=dst: e.tensor_copy(out=dst, in_=src), r=[("ps", b)], w=wk)
        for si, (c0, c1) in enumerate(sg):
            for kc in range(8):
                s.sumsq(kc, si, c0, c1)

    def norm(s, sg, gm_ap, sh_ap, rk, dst_is_x=False):
        s.fence()
        for si, (c0, c1) in enumerate(sg):
            w = c1 - c0
            bs = s.ssb(si)
            s.op("act", lambda e, bs=bs, c0=c0, c1=c1, w=w: e.activation(out=s.rstd[:, c0:c1], in_=s.ps[bs][:, 0:w], func=AF.Sqrt,
                                                                        bias=s.epsc[:, 0:1], scale=1.0 / 1024), r=[("ps", bs), "epsc"], w=[("rstd", si)], u=True)
            s.op("dve", lambda e, c0=c0, c1=c1: e.reciprocal(out=s.rstd[:, c0:c1], in_=s.rstd[:, c0:c1]), r=[("rstd", si)], w=[("rstd", si)], u=True)
            for kc in range(8):
                i = s.ri("ntmp", 4)
                s.op("pool" if kc in (2, 5) else "dve", lambda e, i=i, kc=kc, c0=c0, c1=c1, w=w: e.tensor_tensor(out=s.ntmp[:, i, 0:w], in0=s.xT[:, kc, c0:c1],
                                                                                   in1=s.rstd[:, c0:c1], op=ALU.mult),
                     r=[("xT", kc, si), ("rstd", si)], w=[("ntmp", i)], u=True)
                if dst_is_x:
                    s.op("act", lambda e, i=i, kc=kc, c0=c0, c1=c1, w=w: e.activation(out=s.xT[:, kc, c0:c1], in_=s.ntmp[:, i, 0:w], func=AF.Identity,
                                                                                     scale=gm_ap[:, kc:kc + 1]), r=[("ntmp", i), "prm"], w=[("xT", kc, si)], u=True)
                else:
                    s.op("act", lambda e, i=i, kc=kc, c0=c0, c1=c1, w=w: e.activation(out=s.hb[:, kc, c0:c1], in_=s.ntmp[:, i, 0:w], func=AF.Identity,
                                                                                     scale=gm_ap[:, kc:kc + 1], bias=sh_ap[:, kc:kc + 1]),
                         r=[("ntmp", i)] + rk, w=[("hb", kc, si)], u=True)

    def mix(s, ti, l):
        tok0, W = s.tiles[ti]
        sg = s.segs(ti)
        nsg = len(sg)
        mcol = s.ones_f[:, 0:1]
        hbk = lambda si: [("hb", kc, si) for kc in range(8)]
        s.fence()
        s.op("pool", lambda e: e.tensor_copy(out=s.glu[:, :, 0:30], in_=s.cglu[:, l]), r=["cglu"], w=[("gluh", 0), ("gluh", 1)], u=True)
        s.op("pool", lambda e: e.tensor_copy(out=s.pbx[:, :, 0:16], in_=s.cpb[:, l]), r=["cpb"], w=[("pbxh", 0), ("pbxh", 1)], u=True)
        s.op("pool", lambda e: e.tensor_copy(out=s.m[:, :, 0:2], in_=s.cm[:, l]), r=["cm"], w=[("mh", 0), ("mh", 1)], u=True)
        oD = s.off[f"{l}.cdw"]
        s.op("dve", lambda e: e.tensor_tensor(out=s.diagD, in0=s.ident[:].unsqueeze(1).broadcast_to([128, 6, 128]),
                                              in1=s.prm[:, oD:oD + 6].unsqueeze(2).broadcast_to([128, 6, 128]), op=ALU.mult),
             r=["ident", "prm"], w=["diagD"], u=True)

        def build_diagA(j):
            oA = s.off[f"{l}.caw"] + 31 * j
            s.op("dve", lambda e: e.tensor_tensor(out=s.diagA, in0=s.imb[:].unsqueeze(1).broadcast_to([128, 31, 128]),
                                                  in1=s.prm[:, oA:oA + 31].unsqueeze(2).broadcast_to([128, 31, 128]), op=ALU.mult),
                 r=["imb", "prm"], w=["diagA"], u=True)

        build_diagA(0)

        wsC, wkC = s.use_load()
        wvC = wsC[:, 0:4096].rearrange("p (k n) -> p k n", k=8)

        def sgu(si, iv, nst):
            s.sgu_done.add((ti, l, si))
            s.pend[("vn", iv)] = False
            c0, c1 = sg[si]
            w = c1 - c0
            for j in range(2):
                b = s.bank()
                for stt in range(nst):
                    for hh in range(2):
                        h = 2 * j + hh
                        s.op("pe", lambda e, b=b, stt=stt, hh=hh, h=h: e.matmul(
                            s.ps[b][64 * hh:64 * hh + 64, stt * 128:(stt + 1) * 128], lhsT=s.vn[:, iv, stt, h * 64:(h + 1) * 64],
                            rhs=s.WTm[:, l, h, :], start=True, stop=True), r=[("vn", iv), ("WTm", l)], w=[("ps", b)], u=True)
                s.op("dve", lambda e, b=b, j=j: e.scalar_tensor_tensor(
                    out=s.svb[:, j, c0:c1].rearrange("p (a t) -> p a t", a=nst), in0=s.ps[b][:, 0:w].rearrange("p (a t) -> p a t", a=nst),
                    scalar=s.pc(f"{l}.lng", j), in1=s.Bfull[:, l, j, :].unsqueeze(1).broadcast_to([128, nst, 128]),
                    op0=ALU.mult, op1=ALU.add), r=[("ps", b), "prm", ("Bfull", l, j, 0), ("Bfull", l, j, 1)], w=[("svb", j, si)], u=True)
            s.tk()

        wsA, wkA = s.use_load(hold=1)
        wvA = wsA[:, 0:4096].rearrange("p (k n) -> p k n", k=8)

        def var_A(j, si, i):
            c0, c1 = sg[si]
            w = c1 - c0
            b = s.bank()
            ir = s.ri("rsA")
            s.op("pe", lambda e: e.matmul(s.ps[b][:, 0:w], lhsT=s.blk_b[:], rhs=s.sqA[:, i, 0:w], start=True, stop=True),
                 r=[("sqA", i), "blk_b"], w=[("ps", b)], u=True)
            s.op("act", lambda e: e.activation(out=s.rsA[:, ir, 0:w], in_=s.ps[b][:, 0:w], func=AF.Sqrt, bias=s.epsc[:, 0:1], scale=1.0),
                 r=[("ps", b), "epsc"], w=[("rsA", ir)], u=True)
            s.op("dve", lambda e: e.reciprocal(out=s.rsA[:, ir, 0:w], in_=s.rsA[:, ir, 0:w]), r=[("rsA", ir)], w=[("rsA", ir)], u=True)
            s.op("dve", lambda e: e.tensor_tensor(out=s.cA[:, i, 0:w], in0=s.cA[:, i, 0:w], in1=s.rsA[:, ir, 0:w], op=ALU.mult),
                 r=[("cA", i), ("rsA", ir)], w=[("cA", i)], u=True)
            def fin():
                s.op("act", lambda e: e.activation(out=s.y[:, j, c0:c1], in_=s.cA[:, i, 0:w], func=AF.Silu,
                                                   scale=s.pc(f"{l}.gng", j), bias=s.pc(f"{l}.gnb", j)),
                     r=[("cA", i), "prm"], w=[("y", j, si)], u=True)
                s.pend[("Aunit", i)] = False
            s.defer(2, fin)
            s.tk()

        def conv_A(j, si, i):
            c0, c1 = sg[si]
            w = c1 - c0
            b = s.bank()
            rk = [("glu", j, si), ("glu", j, si - 1) if si > 0 else ("gluh", j), "diagA"]
            s.mmg(s.ps[b][:, 0:w], b, [(s.diagA[:, k, :], s.glu[:, j, c0 + k:c0 + k + w]) for k in range(31)], rk, u=True)
            s.op("act", lambda e: e.activation(out=s.cA[:, i, 0:w], in_=s.ps[b][:, 0:w], func=AF.Identity, bias=s.bcn[:, l, j:j + 1], scale=1.0),
                 r=[("ps", b), ("bcn", l)], w=[("cA", i)], u=True)
            s.op("act", lambda e: e.activation(out=s.sqA[:, i, 0:w], in_=s.ps[b][:, 0:w], func=AF.Square, bias=s.bcn[:, l, j:j + 1], scale=1.0),
                 r=[("ps", b), ("bcn", l)], w=[("sqA", i)], u=True)
            if j == 0 and si == nsg - 1:
                build_diagA(1)
                st8["rebuilt"] = True
                for k, (si2, i2) in enumerate(st8["wait"]):
                    s.defer(5 + 3 * k, lambda si2=si2, i2=i2: conv_A(1, si2, i2))
                st8["wait"] = []
            s.tk(3)
            s.defer(3, lambda: var_A(j, si, i))

        st8 = {"rebuilt": False, "wait": []}
        def c_seg(si, c0, c1):
                w = c1 - c0
                nst = w // 128
                iv = s.ri_wait("vn")
                groups = []
                vbanks = []
                for p0 in range(0, nst):
                    b = s.bank()
                    npair = 1
                    vbanks.append((b, p0, npair))
                    stt = p0
                    groups.append((s.ps[b][:, 0:256], b,
                                   [(s.hb[:, kc, c0 + stt * 128:c0 + (stt + 1) * 128], wvC[:, kc, 0:256], wkC + [("hb", kc, si)]) for kc in range(8)]))
                bgb = []
                for j in range(2):
                    b = s.bank()
                    bgb.append(b)
                    groups.append((s.ps[b][:, 0:w], b, [(wvC[:, kc, 256 + j * 128:256 + (j + 1) * 128], s.hb[:, kc, c0:c1], wkC + [("hb", kc, si)]) for kc in range(8)]))
                s.mmi(groups)
                for (b, p0, npair) in vbanks:
                    s.op("act", lambda e, b=b, p0=p0, npair=npair: e.activation(
                        out=s.vsb[:, p0:p0 + npair, :], in_=s.ps[b][:, 0:npair * 256].rearrange("p (a c) -> p a c", a=npair), func=AF.Copy),
                        r=[("ps", b)], w=[("vsb", p0)], u=True)
                for j in range(2):
                    s.op("act", lambda e, b=bgb[j], j=j, c0=c0, c1=c1, w=w: e.activation(out=s.bgs[:, j, c0:c1], in_=s.ps[b][:, 0:w], func=AF.Copy),
                         r=[("ps", b)], w=[("bgs", j, si)], u=True)
                vk = [("vsb", p0) for p0 in range(0, nst)]
                vsbV, vtmpV, vnV = s.vsb[:, 0:nst, :], s.vtmp[:, 0:nst, :], s.vn[:, iv, 0:nst, :]
                S_ = [s.vst[:, q, 0:nst] for q in range(6)]
                bmean = S_[2].unsqueeze(2).broadcast_to([128, nst, 256])
                brstd = S_[5].unsqueeze(2).broadcast_to([128, nst, 256])
                s.op("dve", lambda e, vsbV=vsbV, vtmpV=vtmpV: e.tensor_tensor(out=vtmpV, in0=vsbV, in1=vsbV, op=ALU.mult), r=vk, w=["vtmp"], u=True)
                s.op("dve", lambda e, vsbV=vsbV, S_=S_: e.tensor_reduce(out=S_[0], in_=vsbV, axis=AX.X, op=ALU.add), r=vk, w=[("vst", 0)], u=True)
                s.op("dve", lambda e, vtmpV=vtmpV, S_=S_: e.tensor_reduce(out=S_[1], in_=vtmpV, axis=AX.X, op=ALU.add), r=["vtmp"], w=[("vst", 1)], u=True)
                s.op("dve", lambda e, S_=S_: e.tensor_scalar(out=S_[2], in0=S_[0], scalar1=1.0 / 256, scalar2=None, op0=ALU.mult),
                     r=[("vst", 0)], w=[("vst", 2)])
                s.op("dve", lambda e, S_=S_: e.tensor_tensor(out=S_[3], in0=S_[2], in1=S_[2], op=ALU.mult), r=[("vst", 2)], w=[("vst", 3)])
                s.op("dve", lambda e, S_=S_: e.scalar_tensor_tensor(out=S_[4], in0=S_[1], scalar=1.0 / 256, in1=S_[3],
                                                             op0=ALU.mult, op1=ALU.subtract), r=[("vst", 1), ("vst", 3)], w=[("vst", 4)])
                s.op("act", lambda e, S_=S_: e.activation(out=S_[5], in_=S_[4], func=AF.Sqrt, bias=s.epsc[:, 0:1], scale=1.0),
                     r=[("vst", 4), "epsc"], w=[("vst", 5)])
                s.op("dve", lambda e, S_=S_: e.reciprocal(out=S_[5], in_=S_[5]), r=[("vst", 5)], w=[("vst", 5)])
                s.op("dve", lambda e, vsbV=vsbV, vtmpV=vtmpV, bmean=bmean: e.tensor_tensor(out=vtmpV, in0=vsbV, in1=bmean, op=ALU.subtract),
                     r=vk + [("vst", 2), "vtmp"], w=["vtmp"], u=True)
                s.op("dve", lambda e, vtmpV=vtmpV, vnV=vnV, brstd=brstd: e.tensor_tensor(out=vnV, in0=vtmpV, in1=brstd, op=ALU.mult),
                     r=["vtmp", ("vst", 5)], w=[("vn", iv)], u=True)
                s.defer(10, lambda si=si, iv=iv, nst=nst: sgu(si, iv, nst))
                s.tk(4)

        def a_seg(si, c0, c1):
                for j in range(2):
                    w = c1 - c0
                    i = s.ri_wait("Aunit", ADEPTH)
                    isg = s.ri("sig")
                    ba = s.bank()
                    s.mmg(s.ps[ba][:, 0:w], ba, [(wvA[:, kc, j * 128:(j + 1) * 128], s.hb[:, kc, c0:c1]) for kc in range(8)], wkA + hbk(si))
                    bg = s.bank()
                    s.mmg(s.ps[bg][:, 0:w], bg, [(wvA[:, kc, (2 + j) * 128:(3 + j) * 128], s.hb[:, kc, c0:c1]) for kc in range(8)], wkA + hbk(si))
                    s.op("act", lambda e, isg=isg, bg=bg, w=w: e.activation(out=s.sig[:, isg, 0:w], in_=s.ps[bg][:, 0:w], func=AF.Sigmoid),
                         r=[("ps", bg)], w=[("sig", isg)], u=True)
                    s.op("dve", lambda e, isg=isg, ba=ba, j=j, c0=c0, c1=c1, w=w: e.tensor_tensor(out=s.glu[:, j, 30 + c0:30 + c1], in0=s.ps[ba][:, 0:w],
                                                                                                in1=s.sig[:, isg, 0:w], op=ALU.mult),
                         r=[("ps", ba), ("sig", isg)], w=[("glu", j, si)], u=True)
                    if ti == 0 and si == 0:
                        s.op("pool", lambda e, j=j, c0=c0, c1=c1: e.tensor_scalar(out=s.glu[:, j, 30:30 + HALO], in0=s.glu[:, j, 30:30 + HALO],
                                                                                  scalar1=s.pc("mask"), scalar2=None, op0=ALU.mult),
                             r=[("glu", j, si), "prm"], w=[("glu", j, si)], u=True)
                    if j == 0 or st8["rebuilt"]:
                        s.defer(3, lambda j=j, si=si, i=i: conv_A(j, si, i))
                    else:
                        st8["wait"].append((si, i))
                    s.tk(2)


        for si, (c0, c1) in enumerate(sg):
            c_seg(si, c0, c1)
            a_seg(si, c0, c1)
        s.maybe_ada(ti, l, ("mix", 0))
        s.maybe_ada(ti, l, ("mix", 1))
        ws, wk = s.use_load()
        wv = ws[:, 0:4096].rearrange("p (k n) -> p k n", k=8)

        def poolmm(j, si, ip):
            c0, c1 = sg[si]
            w = c1 - c0
            b = s.bank()
            s.op("pe", lambda e: e.matmul(s.ps[b][:, 0:w], lhsT=s.PW[:, l, j, :], rhs=s.pooled[:, ip, 0:w], start=True, stop=True),
                 r=[("pooled", ip, 0), ("pooled", ip, 1), ("PW", l)], w=[("ps", b)], u=True)
            s.op("act", lambda e: e.activation(out=s.y[:, 2 + j, c0:c1], in_=s.ps[b][:, 0:w], func=AF.Identity, scale=s.pc(f"{l}.psc", j)),
                 r=[("ps", b), "prm"], w=[("y", 2 + j, si)], u=True)
            s.pend[("pooled", ip)] = False
            s.tk()

        jobs = []

        def pool_job(k):
            j, si, c0, c1 = jobs[k]
            w = c1 - c0
            o = 16 + c0
            xk = [("pbx", j, si), ("pbx", j, si - 1) if si > 0 else ("pbxh", j)]
            x = s.pbx[:, j, :]
            s.op("pool", lambda e: e.tensor_tensor(out=s.sA[:, 2:16 + w], in0=x[:, o - 14:o + w], in1=x[:, o - 15:o + w - 1], op=ALU.add),
                 r=xk, w=["sA"], u=True)
            if j == 0:
                s.op("pool", lambda e: e.tensor_tensor(out=s.sB[64:128, 4:16 + w], in0=s.sA[64:128, 4:16 + w], in1=s.sA[64:128, 2:14 + w], op=ALU.add),
                     r=["sA"], w=["sB"], u=True)
            else:
                s.op("pool", lambda e: e.tensor_tensor(out=s.sB[:, 4:16 + w], in0=s.sA[:, 4:16 + w], in1=s.sA[:, 2:14 + w], op=ALU.add),
                     r=["sA"], w=["sB"], u=True)
                s.op("pool", lambda e: e.tensor_tensor(out=s.sA[:, 8:16 + w], in0=s.sB[:, 8:16 + w], in1=s.sB[:, 4:12 + w], op=ALU.add),
                     r=["sB", "sA"], w=["sA"], u=True)
                s.op("pool", lambda e: e.tensor_tensor(out=s.sB[64:128, 16:16 + w], in0=s.sA[64:128, 16:16 + w], in1=s.sA[64:128, 8:8 + w], op=ALU.add),
                     r=["sA", "sB"], w=["sB"], u=True)
            s.defer(4, lambda: fin_job(k))

        def fin_job(k):
            j, si, c0, c1 = jobs[k]
            w = c1 - c0
            o = 16 + c0
            xk = [("pbx", j, si), ("pbx", j, si - 1) if si > 0 else ("pbxh", j)]
            ip = s.ri_wait("pooled", 4)
            for hh, src in ((0, s.sA), (1, s.sB)):
                rows = slice(64 * hh, 64 * hh + 64)
                s.op("dve", lambda e, rows=rows, src=src: e.scalar_tensor_tensor(
                    out=s.pooled[rows, ip, 0:w], in0=src[rows, 16:16 + w], scalar=s.invw[rows, j:j + 1], in1=s.pbx[rows, j, o:o + w],
                    op0=ALU.mult, op1=ALU.subtract), r=["sA", "sB", "invw"] + xk, w=[("pooled", ip, hh)], u=True)
                if ti == 0 and c0 <= HALO < c1:
                    fo = HALO - c0
                    s.op("dve", lambda e, rows=rows, src=src, fo=fo: e.tensor_tensor(out=s.tmp16[rows, :], in0=src[rows, 16 + fo:32 + fo], in1=s.rdiv16[rows, j, :], op=ALU.mult),
                         r=["sA", "sB", ("rdiv", j)], w=[("tmp16", hh)], u=True)
                    s.op("dve", lambda e, rows=rows, fo=fo: e.tensor_tensor(out=s.pooled[rows, ip, fo:fo + 16], in0=s.tmp16[rows, :], in1=s.pbx[rows, j, o + fo:o + fo + 16], op=ALU.subtract),
                         r=[("tmp16", hh), ("pooled", ip, hh)] + xk, w=[("pooled", ip, hh)], u=True)
            s.defer(2, lambda: poolmm(j, si, ip))
            if k + 1 < len(jobs):
                pool_job(k + 1)
            else:
                st8["prun"] = False

        for j in range(2):
            for si, (c0, c1) in enumerate(sg):
                w = c1 - c0
                b = s.bank()
                s.mmg(s.ps[b][:, 0:w], b, [(wv[:, kc, j * 128:(j + 1) * 128], s.hb[:, kc, c0:c1]) for kc in range(8)], wk + hbk(si))
                s.op("act", lambda e, b=b, j=j, c0=c0, c1=c1, w=w: e.activation(out=s.pbx[:, j, 16 + c0:16 + c1], in_=s.ps[b][:, 0:w], func=AF.Copy),
                     r=[("ps", b)], w=[("pbx", j, si)], u=True)
                if ti == 0 and si == 0:
                    s.op("pool", lambda e, j=j, c0=c0, c1=c1: e.tensor_scalar(out=s.pbx[:, j, 16:16 + HALO], in0=s.pbx[:, j, 16:16 + HALO],
                                                                              scalar1=s.pc("mask"), scalar2=None, op0=ALU.mult),
                         r=[("pbx", j, si), "prm"], w=[("pbx", j, si)], u=True)
                jobs.append((j, si, c0, c1))
                if not st8.get("prun"):
                    st8["prun"] = True
                    s.defer(1, lambda k=len(jobs) - 1: pool_job(k))
                s.tk()
        for j in range(2):
            for si, (c0, c1) in enumerate(sg):
                w = c1 - c0
                b = s.bank()
                s.mmg(s.ps[b][:, 0:w], b, [(wv[:, kc, 256 + j * 128:256 + (j + 1) * 128], s.hb[:, kc, c0:c1]) for kc in range(8)], wk + hbk(si))
                if (ti, l, si) not in s.sgu_done:
                    s.flush()
                s.op("dve", lambda e, b=b, j=j, c0=c0, c1=c1, w=w: e.tensor_tensor(out=s.y[:, 4 + j, c0:c1], in0=s.ps[b][:, 0:w], in1=s.svb[:, j, c0:c1], op=ALU.mult),
                     r=[("ps", b), ("svb", j, si)], w=[("y", 4 + j, si)], u=True)
                s.tk()

        s.maybe_ada(ti, l, ("mix", 2))
        ws, wk = s.use_load()
        wv = ws[:, 0:4096].rearrange("p (k n) -> p k n", k=8)

        def conv_D(j, si):
            c0, c1 = sg[si]
            w = c1 - c0
            b = s.bank()
            rk = [("m", j, si), ("m", j, si - 1) if si > 0 else ("mh", j), "diagD"]
            s.mmg(s.ps[b][:, 0:w], b, [(s.diagD[:, j * 3 + k, :], s.m[:, j, c0 + k:c0 + k + w]) for k in range(3)], rk, u=True)
            s.op("dve", lambda e: e.tensor_tensor(out=s.y[:, 6 + j, c0:c1], in0=s.ps[b][:, 0:w], in1=s.bgs[:, j, c0:c1], op=ALU.mult),
                 r=[("ps", b), ("bgs", j, si)], w=[("y", 6 + j, si)], u=True)
            s.tk()

        for j in range(2):
            for si, (c0, c1) in enumerate(sg):
                w = c1 - c0
                i = s.ri("hhs")
                bc_ = s.bank()
                s.mmg(s.ps[bc_][:, 0:w], bc_, [(wv[:, kc, j * 128:(j + 1) * 128], s.hb[:, kc, c0:c1]) for kc in range(8)], wk + hbk(si))
                bh = s.bank()
                s.mmg(s.ps[bh][:, 0:w], bh, [(wv[:, kc, 256 + j * 128:256 + (j + 1) * 128], s.hb[:, kc, c0:c1]) for kc in range(8)], wk + hbk(si))
                s.op("act", lambda e, i=i, bh=bh, w=w: e.activation(out=s.hhs[:, i, 0:w], in_=s.ps[bh][:, 0:w], func=AF.Copy),
                     r=[("ps", bh)], w=[("hhs", i)], u=True)
                s.op("dve", lambda e, i=i, bc_=bc_, j=j, c0=c0, c1=c1, w=w: e.tensor_tensor(out=s.m[:, j, 2 + c0:2 + c1], in0=s.ps[bc_][:, 0:w], in1=s.hhs[:, i, 0:w], op=ALU.mult),
                     r=[("ps", bc_), ("hhs", i)], w=[("m", j, si)], u=True)
                if ti == 0 and si == 0:
                    s.op("pool", lambda e, j=j, c0=c0, c1=c1: e.tensor_scalar(out=s.m[:, j, 2:2 + HALO], in0=s.m[:, j, 2:2 + HALO],
                                                                              scalar1=s.pc("mask"), scalar2=None, op0=ALU.mult),
                         r=[("m", j, si), "prm"], w=[("m", j, si)], u=True)
                s.defer(3, lambda j=j, si=si: conv_D(j, si))
                s.tk(2)
        s.maybe_ada(ti, l, ("mix", 3))
        s.flush()
        last = nsg - 1
        s.op("pool", lambda e: e.tensor_scalar(out=s.cglu[:, l], in0=s.glu[:, :, W:W + 30], scalar1=mcol, scalar2=None, op0=ALU.mult),
             r=[("glu", 0, last), ("glu", 1, last), "prm"], w=["cglu"], u=True)
        s.op("pool", lambda e: e.tensor_scalar(out=s.cpb[:, l], in0=s.pbx[:, :, W:W + 16], scalar1=mcol, scalar2=None, op0=ALU.mult),
             r=[("pbx", 0, last), ("pbx", 1, last), "prm"], w=["cpb"], u=True)
        s.op("pool", lambda e: e.tensor_scalar(out=s.cm[:, l], in0=s.m[:, :, W:W + 2], scalar1=mcol, scalar2=None, op0=ALU.mult),
             r=[("m", 0, last), ("m", 1, last), "prm"], w=["cm"], u=True)

    def wout(s, ti, l):
        tok0, W = s.tiles[ti]
        sg = s.segs(ti)
        for g in range(2):
            ws, wk = s.use_load()
            wv = ws[:, 0:4096].rearrange("p (k n) -> p k n", k=8)
            for mm in range(4):
                m = 4 * g + mm
                for si, (c0, c1) in enumerate(sg):
                    w = c1 - c0
                    b = s.bank()
                    s.mmg(s.ps[b][:, 0:w], b, [(wv[:, kc, mm * 128:(mm + 1) * 128], s.y[:, kc, c0:c1]) for kc in range(8)],
                          wk + [("y", kc, si) for kc in range(8)], u=True)
                    s.op("dve", lambda e, b=b, m=m, c0=c0, c1=c1, w=w: e.scalar_tensor_tensor(
                        out=s.xT[:, m, c0:c1], in0=s.ps[b][:, 0:w], scalar=s.modT[:, l, 16 + m:17 + m], in1=s.xT[:, m, c0:c1],
                        op0=ALU.mult, op1=ALU.add), r=[("ps", b), ("modT", l), ("xT", m, si)], w=[("xT", m, si)])
                    s.defer(4, lambda m=m, si=si, c0=c0, c1=c1: s.sumsq(m, si, c0, c1))
                    s.tk()
            s.maybe_ada(ti, l, ("wout", g))
        s.flush()

    def ffn(s, ti, l):
        tok0, W = s.tiles[ti]
        sg = s.segs(ti)
        nsg = len(sg)
        mcol = s.ones_f[:, 0:1]
        oF = s.off[f"{l}.fcw"]
        s.fence()
        tail = [None]

        def make_g(g, hold=0):
            ws, wk = s.use_load(hold=hold)
            wv = ws[:, 0:4096].rearrange("p (k n) -> p k n", k=8)
            ii = [s.ri("G", 4), s.ri("G", 4)]
            for jj in range(2):
                j = 2 * g + jj
                s.op("act", lambda e, i=ii[jj], j=j: e.activation(out=s.G[:, i, 0:2], in_=s.cG[:, l, j, :], func=AF.Copy),
                     r=[("cG", j)], w=[("Gh", ii[jj])], u=True)
            def grp(jj, bank_, c0, c1, si, up):
                co = (256 if up else 0) + jj * 128
                return (s.ps[bank_][:, 0:c1 - c0], bank_, [(wv[:, kc, co:co + 128], s.hb[:, kc, c0:c1], wk + [("hb", kc, si)]) for kc in range(8)])

            def consume(jj, si, c0, c1, w, bg, bu):
                j = 2 * g + jj
                i = ii[jj]
                i2 = s.ri("Funit", 4)
                gk = [("G", i, si), ("G", i, si - 1) if si > 0 else ("Gh", i)]
                s.op("act", lambda e, i=i, bg=bg, c0=c0, c1=c1, w=w: e.activation(out=s.G[:, i, 2 + c0:2 + c1], in_=s.ps[bg][:, 0:w], func=AF.Copy),
                     r=[("ps", bg)], w=[("G", i, si)], u=True)
                if ti == 0 and si == 0:
                    s.op("pool", lambda e, i=i: e.tensor_scalar(out=s.G[:, i, 2:2 + HALO], in0=s.G[:, i, 2:2 + HALO],
                                                                scalar1=s.pc("mask"), scalar2=None, op0=ALU.mult),
                         r=[("G", i, si), "prm"], w=[("G", i, si)], u=True)
                s.op("act", lambda e, i2=i2, bg=bg, j=j, w=w: e.activation(out=s.G2[:, i2, 0:w], in_=s.ps[bg][:, 0:w], func=AF.Identity,
                                                                         scale=s.prm[:, oF + 44 + j:oF + 45 + j]),
                     r=[("ps", bg), "prm"], w=[("G2", i2)], u=True)
                if si == nsg - 1:
                    s.op("act", lambda e, bg=bg, j=j, w=w: e.activation(out=s.cG[:, l, j, :], in_=s.ps[bg][:, w - 2:w], func=AF.Identity, scale=mcol),
                         r=[("ps", bg), "prm"], w=[("cG", j)])
                s.op("dve", lambda e, i=i, i2=i2, j=j, c0=c0, c1=c1, w=w: e.scalar_tensor_tensor(
                    out=s.ft[:, i2, 0:w], in0=s.G[:, i, 1 + c0:1 + c1], scalar=s.prm[:, oF + 22 + j:oF + 23 + j], in1=s.G2[:, i2, 0:w],
                    op0=ALU.mult, op1=ALU.add), r=gk + [("G2", i2), "prm"], w=[("ft", i2)], u=True)
                s.op("dve", lambda e, i=i, i2=i2, j=j, c0=c0, c1=c1, w=w: e.scalar_tensor_tensor(
                    out=s.ft[:, i2, 0:w], in0=s.G[:, i, c0:c1], scalar=s.prm[:, oF + j:oF + j + 1], in1=s.ft[:, i2, 0:w],
                    op0=ALU.mult, op1=ALU.add), r=gk + [("ft", i2), "prm"], w=[("ft", i2)], u=True)
                if tail[0] is not None:
                    tail[0]()

                def mk_tail(i2=i2, bu=bu, j=j, c0=c0, c1=c1, w=w, si=si):
                    def t():
                        s.op("act", lambda e: e.activation(out=s.fs[:, i2, 0:w], in_=s.ft[:, i2, 0:w], func=AF.Silu),
                             r=[("ft", i2)], w=[("fs", i2)], u=True)
                        s.op("dve", lambda e: e.tensor_tensor(out=s.a[:, j, c0:c1], in0=s.ps[bu][:, 0:w], in1=s.fs[:, i2, 0:w], op=ALU.mult),
                             r=[("ps", bu), ("fs", i2)], w=[("a", j, si)], u=True)
                    return t
                tail[0] = mk_tail()

            def do_seg(si, block):
                c0, c1 = sg[si]
                w = c1 - c0
                if block:
                    bks = [(s.bank(), s.bank()) for _ in range(2)]
                    s.mmi([grp(jj, bks[jj][u_], c0, c1, si, bool(u_)) for jj in range(2) for u_ in range(2)])
                    for jj in range(2):
                        consume(jj, si, c0, c1, w, *bks[jj])
                    s.tk(4)
                else:
                    for jj in range(2):
                        bg, bu = s.bank(), s.bank()
                        s.mmi([grp(jj, bg, c0, c1, si, False), grp(jj, bu, c0, c1, si, True)])
                        consume(jj, si, c0, c1, w, bg, bu)
                        s.tk(2)
            return do_seg

        d0, d1 = make_g(0), make_g(1, hold=1)
        d0(0, True)
        d1(0, False)
        for si in range(1, nsg):
            d0(si, False)
            d1(si, False)
        s.maybe_ada(ti, l, ("ffn", 0))
        s.maybe_ada(ti, l, ("ffn", 1))
        for g in range(2, 11):
            d = make_g(g)
            for si in range(nsg):
                d(si, False)
            s.maybe_ada(ti, l, ("ffn", g))
        if tail[0] is not None:
            tail[0]()

    def down(s, ti, l):
        tok0, W = s.tiles[ti]
        sg = s.segs(ti)
        for g in range(4):
            ws, wk = s.use_load()
            wv = ws[:, 0:5632].rearrange("p (k n) -> p k n", k=22)
            for mm in range(2):
                m = 2 * g + mm
                for si, (c0, c1) in enumerate(sg):
                    w = c1 - c0
                    b = s.bank()
                    s.mmg(s.ps[b][:, 0:w], b, [(wv[:, kc, mm * 128:(mm + 1) * 128], s.a[:, kc, c0:c1]) for kc in range(22)],
                          wk + [("a", kc, si) for kc in range(22)], u=True)
                    s.op("dve", lambda e, b=b, m=m, c0=c0, c1=c1, w=w: e.scalar_tensor_tensor(
                        out=s.xT[:, m, c0:c1], in0=s.ps[b][:, 0:w], scalar=s.modT[:, l, 40 + m:41 + m], in1=s.xT[:, m, c0:c1],
                        op0=ALU.mult, op1=ALU.add), r=[("ps", b), ("modT", l), ("xT", m, si)], w=[("xT", m, si)])
                    s.defer(4, lambda m=m, si=si, c0=c0, c1=c1: s.sumsq(m, si, c0, c1))
                    s.tk(3)
            s.maybe_ada(ti, l, ("down", g))
        s.flush()

    def store(s, ti):
        tok0, W = s.tiles[ti]
        sg = s.segs(ti)
        for stt in range(1 if ti == 0 else 0, W // 128):
            i = s.ri("stage")
            si = s.seg_of(sg, stt * 128)
            for half in range(2):
                b = s.bank()
                for q in range(4):
                    kc = half * 4 + q
                    s.op("pe", lambda e, b=b, q=q, kc=kc, stt=stt: e.transpose(out=s.ps[b][:, q * 128:(q + 1) * 128],
                                                                            in_=s.xT[:, kc, stt * 128:(stt + 1) * 128], identity=s.ident[:]),
                         r=[("xT", kc, si), "ident"], w=[("ps", b)])
                if half == 0:
                    s.op("act", lambda e, b=b, i=i: e.activation(out=s.stage[:, i, 0:512], in_=s.ps[b][:], func=AF.Copy),
                         r=[("ps", b)], w=[("stage", i)], u=True)
                else:
                    s.op("dve", lambda e, b=b, i=i: e.tensor_copy(out=s.stage[:, i, 512:1024], in_=s.ps[b][:]),
                         r=[("ps", b), ("stage", i)], w=[("stage", i)], u=True)
            r0 = tok0 - HALO + stt * 128
            s.P.add("sp", lambda e, i=i, r0=r0: e.dma_start(out=s.dOut[r0:r0 + 128, :], in_=s.stage[:, i, :]),
                    reads=[("stage", i), "U"], writes=[("out", i)], dma=f"st{i}")

    def run(s):
        L = s.L
        s.plan_loads()
        for i in range(2):
            s.P.dma_sem(f"st{i}")
        s.P.phase = "setup"
        s.setup()
        for ti in range(len(s.tiles)):
            tok0, W = s.tiles[ti]
            sg = s.segs(ti)
            s.nsg = len(sg)
            s.nrot = 8 - s.nsg
            s.P.phase = f"t{ti}.load"
            s.fence()
            s.load_x(ti)
            if ti == 0:
                s.P.phase = "ada0"
                for g in range(12):
                    s.ada_group(0, g)
            for l in range(L):
                s.P.phase = f"t{ti}.l{l}.norm1"
                s.norm(sg, s.gm[:, l, 0, :], s.modT[:, l, 0:8], [("gm", l, 0), ("modT", l)])
                s.P.phase = f"t{ti}.l{l}.mix"
                s.nrot = 8
                s.mix(ti, l)
                s.nrot = 8 - s.nsg
                s.P.phase = f"t{ti}.l{l}.wout"
                s.wout(ti, l)
                s.P.phase = f"t{ti}.l{l}.norm2"
                s.norm(sg, s.gm[:, l, 1, :], s.modT[:, l, 24:32], [("gm", l, 1), ("modT", l)])
                s.P.phase = f"t{ti}.l{l}.ffn"
                s.nrot = 8
                s.ffn(ti, l)
                s.nrot = 8 - s.nsg
                s.P.phase = f"t{ti}.l{l}.down"
                s.down(ti, l)
            s.P.phase = f"t{ti}.final"
            if True:
                if s.final:
                    s.norm(sg, s.pc("fg", 0, 8), None, ["prm"], dst_is_x=True)
                s.fence()
                s.store(ti)
            else:
                pass
        s.P.add("sp", None, reads=[("out", 0), ("out", 1)])
        return s.P.emit()


def build_nc(L, ntok, TW, final=True):
    nc = bass.Bass("TRN2", target_bir_lowering=False)
    with ExitStack() as st:
        st.enter_context(nc.allow_low_precision("bf16 matmul operands, fp32 accumulation"))
        k = Kern(nc, st, L, ntok, TW, final)
        info = k.run()
    info["prog"] = k.P
    return nc, info


def pack_inputs(inp, L, S, ncores, final=True):
    B = inp["x"].shape[0]
    halves = ncores // B
    ntok = S // halves
    off, NPRM = prm_layout(L)
    f = lambda a: np.asarray(a, dtype=np.float32)
    tabs = np.zeros((128, L, 1024), np.float32)
    base = np.zeros((128, NPRM), np.float32)
    for l in range(L):
        def put(nm, arr):
            base[:, off[f"{l}.{nm}"]: off[f"{l}.{nm}"] + arr.shape[1]] = arr
        put("n1g", vec_cols(f(inp["norm1_g"])[l]))
        put("n2g", vec_cols(f(inp["norm2_g"])[l]))
        put("cab", vec_cols(f(inp["conv_a_b"])[l]))
        put("adab", vec_cols(f(inp["ada_b"])[l]))
        put("gng", vec_cols(f(inp["gn_a_g"])[l]))
        put("gnb", vec_cols(f(inp["gn_a_b"])[l]))
        put("psc", vec_cols(f(inp["pool_scale"])[l]))
        put("lng", vec_cols(f(inp["sgu_ln_g"])[l]))
        put("lnb", vec_cols(f(inp["sgu_ln_b"])[l]))
        caw = f(inp["conv_a_w"])[l]
        put("caw", np.ascontiguousarray(caw.T.reshape(2, 128, 31).transpose(1, 0, 2).reshape(128, 62)))
        cdw = f(inp["conv_d_w"])[l]
        put("cdw", np.ascontiguousarray(cdw.T.reshape(2, 128, 3).transpose(1, 0, 2).reshape(128, 6)))
        fcw = f(inp["ffn_conv_w"])[l]
        put("fcw", np.ascontiguousarray(fcw.reshape(3, 22, 128).transpose(2, 0, 1).reshape(128, 66)))
        sb = f(inp["sgu_b"])[l]
        for j in range(2):
            for hh in range(2):
                tabs[64 * hh:64 * hh + 64, l, j * 128:(j + 1) * 128] = sb[2 * j + hh][None, :]
        pw = f(inp["pool_w"])[l]
        for j in range(2):
            for hh in range(2):
                tabs[64 * hh:64 * hh + 64, l, 256 + j * 128 + 64 * hh: 256 + j * 128 + 64 * hh + 64] = pw[2 * j + hh]
        sw = f(inp["sgu_w"])[l]
        tabs[:, l, 512:1024] = sw.transpose(2, 0, 1).reshape(128, 512)
    base[:, off["fg"]:off["fg"] + 8] = vec_cols(f(inp["final_g"]))
    maps = []
    x = f(inp["x"])
    shared = dict(tabs=tabs.reshape(128, L * 1024), ada_w=f(inp["ada_w"])[:L], w_in=f(inp["w_in"])[:L],
                  w_out=f(inp["w_out"])[:L], ffn_w_gate=f(inp["ffn_w_gate"])[:L], ffn_w_up=f(inp["ffn_w_up"])[:L],
                  ffn_w_down=f(inp["ffn_w_down"])[:L])
    for c in range(ncores):
        b, hf = divmod(c, halves)
        t0 = hf * ntok
        xs = np.zeros((HALO + ntok, 1024), np.float32)
        xs[HALO:] = x[b, t0:t0 + ntok]
        if hf > 0:
            xs[:HALO] = x[b, t0 - HALO:t0]
        prm = base.copy()
        prm[:, off["c"]:off["c"] + 8] = vec_cols(f(inp["c"])[b])
        prm[:, off["mask"]] = 1.0 if hf > 0 else 0.0
        prm[:, off["pos"]:off["pos"] + 16] = (t0 + 1 + np.arange(16, dtype=np.float32))[None, :]
        d = dict(shared)
        d["xs"] = xs
        d["prm"] = prm
        maps.append(d)
    return maps, ntok, halves


_CACHE = {}


def run_model(inp, L, TW=1024, final=True, trace=False):
    B, S, _ = inp["x"].shape
    ncores = 8
    maps, ntok, halves = pack_inputs(inp, L, S, ncores, final)
    key = (L, ntok, TW, final)
    if key not in _CACHE:
        _CACHE[key] = build_nc(L, ntok, TW, final)
    nc, info = _CACHE[key]
    res = run_bass_kernel_spmd(nc, maps, core_ids=list(range(ncores)), **({"trace": True} if trace else {}))
    out = np.zeros((B, S, 1024), np.float32)
    for c in range(ncores):
        b, hf = divmod(c, halves)
        out[b, hf * ntok:(hf + 1) * ntok] = res.results[c]["out"]
    return out, res


def kernel(**inputs):
    out, _ = run_model(inputs, L=2, TW=1024, final=True)
    return out
```

```python
import numpy as np
from contextlib import ExitStack
import concourse.bass as bass
import concourse.mybir as mybir
from concourse.bass_utils import run_bass_kernel_spmd

F32 = mybir.dt.float32
BF16 = mybir.dt.bfloat16
AF = mybir.ActivationFunctionType
ALU = mybir.AluOpType
AX = mybir.AxisListType
EPS = 1e-6
HALO = 128
NS = 3
SLOT = 5632
DEBUG_LABELS = False
ADA_POS = [("mix", 0), ("mix", 1), ("mix", 2), ("mix", 3), ("wout", 0), ("wout", 1),
           ("ffn", 1), ("ffn", 3), ("ffn", 5), ("ffn", 7), ("ffn", 9), ("down", 0)]
ADEPTH = 4
NROT = 6


class Prog:
    def __init__(self, nc, stack):
        self.nc = nc
        self.stack = stack
        self.ops = []
        self.lastw = {}
        self.readers = {}
        self.engs = {"pe": nc.tensor, "act": nc.scalar, "dve": nc.vector, "pool": nc.gpsimd, "sp": nc.sync}
        self.esem = {e: stack.enter_context(nc.semaphore("s_" + e)) for e in self.engs}
        self.dsems = {}

    def dma_sem(self, name):
        if name not in self.dsems:
            self.dsems[name] = self.stack.enter_context(self.nc.semaphore("d_" + name))
        return name

    def add(self, eng, fn, reads=(), writes=(), dma=None):
        idx = len(self.ops)
        deps = {}
        for k in reads:
            p = self.lastw.get(k)
            if p is not None:
                deps[p] = "raw"
        for k in writes:
            p = self.lastw.get(k)
            if p is not None and p not in deps:
                deps[p] = "waw"
            for r in self.readers.get(k, ()):
                if r not in deps:
                    deps[r] = "war"
        self.ops.append(dict(eng=eng, fn=fn, deps=deps, dma=dma, sig=False, ph=getattr(self, "phase", "")))
        for k in reads:
            self.readers.setdefault(k, []).append(idx)
        for k in writes:
            self.lastw[k] = idx
            self.readers[k] = []
        return idx

    @staticmethod
    def _needed(c, p, kind):
        if p["dma"] is not None:
            return True
        if c["dma"] is None and c["eng"] == p["eng"]:
            if c["eng"] == "pe":
                return False
            return kind == "raw"
        return True

    def emit(self):
        ops = self.ops
        for c in ops:
            keep = {}
            for p_i, kind in c["deps"].items():
                p = ops[p_i]
                if not self._needed(c, p, kind):
                    continue
                key = ("d", p["dma"]) if p["dma"] is not None else ("e", p["eng"])
                if key not in keep or keep[key] < p_i:
                    keep[key] = p_i
            c["wdeps"] = list(keep.values())
            c["deps"] = None
            for p_i in c["wdeps"]:
                ops[p_i]["sig"] = True
        ecount = {e: 0 for e in self.engs}
        dcount = {d: 0 for d in self.dsems}
        for o in ops:
            if o["dma"] is not None:
                dcount[o["dma"]] += 1
                o["val"] = 16 * dcount[o["dma"]]
            elif o["sig"]:
                ecount[o["eng"]] += 1
                o["val"] = ecount[o["eng"]]
        waited = {e: {} for e in self.engs}
        nwait = 0
        for o in ops:
            eng = self.engs[o["eng"]]
            w = waited[o["eng"]]
            for p_i in o["wdeps"]:
                p = ops[p_i]
                if p["dma"] is not None:
                    sname, sem = ("d", p["dma"]), self.dsems[p["dma"]]
                else:
                    sname, sem = ("e", p["eng"]), self.esem[p["eng"]]
                if w.get(sname, 0) < p["val"]:
                    eng.wait_ge(sem, p["val"])
                    w[sname] = p["val"]
                    nwait += 1
            if o["fn"] is not None:
                inst = o["fn"](eng)
                try:
                    o["iname"] = inst.ins.name
                except Exception:
                    o["iname"] = None
                if o["dma"] is not None:
                    inst.then_inc(self.dsems[o["dma"]], 16)
                elif o["sig"]:
                    inst.then_inc(self.esem[o["eng"]], 1)
        return dict(nops=len(ops), nwait=nwait, ecount=ecount, dcount=dcount)


def prm_layout(L):
    off = {}
    n = 0

    def add(name, w):
        nonlocal n
        off[name] = n
        n += w

    for l in range(L):
        for nm, w in (("n1g", 8), ("n2g", 8), ("cab", 2), ("gng", 2), ("gnb", 2), ("psc", 2), ("lng", 2), ("lnb", 2),
                      ("caw", 62), ("cdw", 6), ("fcw", 66), ("adab", 48)):
            add(f"{l}.{nm}", w)
    add("fg", 8)
    add("c", 8)
    add("mask", 1)
    add("pos", 16)
    return off, n


def vec_cols(v):
    return np.ascontiguousarray(v.reshape(-1, 128).T)


def make_tiles(ntok, tw):
    tiles = []
    t = 0
    while t < HALO + ntok:
        w = min(tw + (HALO if t == 0 else 0), HALO + ntok - t)
        tiles.append((t, w))
        t += w
    return tiles


class Kern:
    def __init__(s, nc, st, L, ntok, TW, final=True):
        s.nc, s.st, s.L, s.ntok, s.final = nc, st, L, ntok, final
        s.NT = HALO + ntok
        s.tiles = make_tiles(ntok, TW)
        s.TW = max(w for _, w in s.tiles)
        s.nrot = NROT
        s.P = Prog(nc, st)
        s.off, s.NPRM = prm_layout(L)
        s.nb = 0
        s.tick = 0
        s.seq = 0
        s.dq = []
        s.rot = {}
        s.ssc = {}
        s.sgu_done = set()
        s.nsg = 2
        s.pend = {}
        dt = nc.dram_tensor
        s.dX = dt("xs", [s.NT, 1024], F32, kind="ExternalInput").ap()
        s.dPrm = dt("prm", [128, s.NPRM], F32, kind="ExternalInput").ap()
        s.dTabs = dt("tabs", [128, L * 1024], F32, kind="ExternalInput").ap()
        s.dAdaW = dt("ada_w", [L, 1024, 6144], F32, kind="ExternalInput").ap()
        s.dWin = dt("w_in", [L, 1024, 2048], F32, kind="ExternalInput").ap()
        s.dWout = dt("w_out", [L, 1024, 1024], F32, kind="ExternalInput").ap()
        s.dWg = dt("ffn_w_gate", [L, 1024, 2816], F32, kind="ExternalInput").ap()
        s.dWu = dt("ffn_w_up", [L, 1024, 2816], F32, kind="ExternalInput").ap()
        s.dWd = dt("ffn_w_down", [L, 2816, 1024], F32, kind="ExternalInput").ap()
        s.dOut = dt("out", [ntok, 1024], F32, kind="ExternalOutput").ap()
        s.alloc()

    def sb(s, name, shape, dtype):
        return s.st.enter_context(s.nc.sbuf_tensor("sb_" + name, shape, dtype))

    def alloc(s):
        nc, L, TW = s.nc, s.L, s.TW
        s.prm = s.sb("prm", [128, s.NPRM], F32)
        s.xT = s.sb("xT", [128, 8, TW], F32)
        s.hb = s.sb("hb", [128, 8, TW], BF16)
        s.ws = [s.sb(f"ws{i}", [128, SLOT], BF16) for i in range(NS)]
        s.sqb = s.sb("sqb", [128, 2, 512], BF16)
        s.ident = s.sb("ident", [128, 128], F32)
        s.ones_f = s.sb("ones_f", [128, 128], F32)
        s.blk = s.sb("blk", [128, 128], F32)
        s.imb = s.sb("imb", [128, 128], F32)
        s.bcn = s.sb("bcn", [128, L, 2], F32)
        s.ones_b = s.sb("ones_b", [128, 128], BF16)
        s.epsc = s.sb("epsc", [128, 1], F32)
        s.modT = s.sb("modT", [128, L, 48], F32)
        s.gm = s.sb("gm", [128, L, 2, 8], F32)
        s.WTm = s.sb("WTm", [128, L, 4, 128], BF16)
        s.PW = s.sb("PW", [128, L, 2, 128], BF16)
        s.Bfull = s.sb("Bfull", [128, L, 2, 128], F32)
        s.cact = s.sb("cact", [128, 8], F32)
        s.cact_b = s.sb("cact_b", [128, 8], BF16)
        s.blk_b = s.sb("blk_b", [128, 128], BF16)
        s.wcol = s.sb("wcol", [128, 2], F32)
        s.invw = s.sb("invw", [128, 2], F32)
        s.rdiv16 = s.sb("rdiv16", [128, 2, 16], F32)
        s.tmp16 = s.sb("tmp16", [128, 16], F32)
        s.cglu = s.sb("cglu", [128, L, 2, 30], BF16)
        s.cpb = s.sb("cpb", [128, L, 2, 16], F32)
        s.cm = s.sb("cm", [128, L, 2, 2], BF16)
        s.cG = s.sb("cG", [128, L, 22, 2], F32)
        s.fz = s.sb("fz", [128, 4], F32)
        s.vst = s.sb("vst", [128, 6, 4], F32)
        s.ps = [s.st.enter_context(nc.psum_tensor(f"ps{b}", [128, 512], F32)) for b in range(8)]
        sizes = {}

        class UA:
            def __init__(u):
                u.o = 0
                u.items = []

            def get(u, name, shape, dtype, parts=128):
                esz = 4 if dtype == F32 else 2
                n = int(np.prod(shape[1:])) * esz
                n = (n + 3) // 4 * 4
                u.items.append((name, u.o, shape, dtype))
                u.o += n

        S_, M_, F_, N_ = UA(), UA(), UA(), UA()
        N_.get("stage", [128, 4, 1024], F32)
        N_.get("rstd", [128, TW], F32)
        N_.get("ntmp", [128, 4, 512], F32)
        S_.get("tabs", [128, L, 1024], F32)
        M_.get("y", [128, 8, TW], BF16)
        M_.get("glu", [128, 2, 32 + TW], BF16)
        M_.get("diagA", [128, 31, 128], BF16)
        M_.get("diagD", [128, 6, 128], BF16)
        for nm in ("sig", "hhs"):
            M_.get(nm, [128, 2, 512], F32)
        M_.get("cA", [128, ADEPTH, 512], F32)
        M_.get("sqA", [128, ADEPTH, 512], BF16)
        M_.get("rsA", [128, 2, 512], F32)
        M_.get("pbx", [128, 2, 16 + TW], F32)
        M_.get("sA", [128, 528], F32)
        M_.get("sB", [128, 528], F32)
        M_.get("pooled", [128, 4, 512], BF16)
        M_.get("vsb", [128, 4, 256], F32)
        M_.get("vtmp", [128, 4, 256], F32)
        M_.get("vn", [128, 2, 4, 256], BF16)
        M_.get("svb", [128, 2, TW], F32)
        M_.get("bgs", [128, 2, TW], F32)
        M_.get("m", [128, 2, 2 + TW], BF16)
        F_.get("a", [128, 22, TW], BF16)
        F_.get("G", [128, 2, 2 + TW], F32)
        for nm in ("G2", "ft", "fs"):
            F_.get(nm, [128, 4, 512], F32)
        usz = max(S_.o, M_.o, F_.o, N_.o)
        s.U = s.sb("U", [128, usz // 2], BF16)
        for ua in (S_, M_, F_, N_):
            for (name, o, shape, dtype) in ua.items:
                esz = 4 if dtype == F32 else 2
                n = int(np.prod(shape[1:]))
                ap = s.U[0:shape[0], o // 2: o // 2 + n * esz // 2]
                if dtype == F32:
                    ap = ap.bitcast(F32)
                if len(shape) == 3:
                    ap = ap.rearrange("p (a b) -> p a b", a=shape[1])
                elif len(shape) == 4:
                    ap = ap.rearrange("p (a b c) -> p a b c", a=shape[1], b=shape[2])
                setattr(s, name, ap)
        print("sbuf bytes remaining per partition:", nc.sbuf_bytes_remaining)

    def op(s, eng, fn, r=(), w=(), u=False):
        r = list(r)
        if u:
            r.append("U")
        idx = s.P.add(eng, fn, reads=r, writes=list(w))
        if DEBUG_LABELS:
            import sys as _sys
            f = _sys._getframe(1)
            if f.f_code.co_name == "mmg":
                f = f.f_back
            s.P.ops[idx]["lab"] = f"{f.f_code.co_name}:{f.f_lineno} w={list(w)[:1]}"

    def fence(s):
        s.P.add("dve", lambda e: e.memset(s.fz[:, 0:1], 0.0), writes=["U"])

    def bank(s):
        b = s.nb % s.nrot
        s.nb += 1
        return b

    def ri(s, name, n=2):
        v = s.rot.get(name, 0)
        s.rot[name] = v + 1
        return v % n

    def ri_wait(s, name, n=2):
        i = s.ri(name, n)
        while s.pend.get((name, i)):
            s.tk()
        s.pend[(name, i)] = True
        return i

    def pc(s, name, j=0, n=1):
        o = s.off[name] + j
        return s.prm[:, o:o + n]

    def defer(s, d, fn):
        s.dq.append((s.tick + d, s.seq, fn))
        s.seq += 1

    def run_due(s):
        while True:
            due = [x for x in s.dq if x[0] <= s.tick]
            if not due:
                return
            x = min(due, key=lambda z: (z[0], z[1]))
            s.dq.remove(x)
            x[2]()

    def tk(s, n=1):
        for _ in range(n):
            s.tick += 1
            s.run_due()

    def flush(s):
        while s.dq:
            s.tick += 1
            s.run_due()

    def mmg(s, out_ap, b, pairs, rkeys, u=False):
        n = len(pairs)
        for i, (l, r) in enumerate(pairs):
            s.op("pe", lambda e, l=l, r=r, i=i: e.matmul(out_ap, lhsT=l, rhs=r, start=(i == 0), stop=(i == n - 1)),
                 r=rkeys, w=[("ps", b)], u=u)

    def mmi(s, groups, u=False):
        n = max(len(g[2]) for g in groups)
        for k in range(n):
            for (out_ap, b, pairs) in groups:
                if k >= len(pairs):
                    continue
                l, r, keys = pairs[k]
                last = len(pairs) - 1
                s.op("pe", lambda e, out_ap=out_ap, l=l, r=r, k=k, last=last: e.matmul(out_ap, lhsT=l, rhs=r, start=(k == 0), stop=(k == last)),
                     r=keys, w=[("ps", b)], u=u)

    def plan_loads(s):
        s.loads = []
        adal = lambda l_: [[((8, 512, 0, 512), s.dAdaW[l_].rearrange("(kc p) n -> p kc n", p=128)[:, :, g_ * 512:(g_ + 1) * 512])] for g_ in range(12)]
        s.loads += adal(0)
        for ti in range(len(s.tiles)):
            for l in range(s.L):
                bg_ada = (ti == 0 and l + 1 < s.L)

                def bga(pos):
                    if bg_ada and pos in ADA_POS:
                        s.loads.append(adal(l + 1)[ADA_POS.index(pos)])
                win = s.dWin[l].rearrange("(kc p) n -> p kc n", p=128)
                for k_, c0 in enumerate((1024, 0, 512, 1536)):
                    s.loads.append([((8, 512, 0, 512), win[:, :, c0:c0 + 512])])
                    bga(("mix", k_))
                wo = s.dWout[l].rearrange("(kc p) n -> p kc n", p=128)
                for g in range(2):
                    s.loads.append([((8, 512, 0, 512), wo[:, :, g * 512:(g + 1) * 512])])
                    bga(("wout", g))
                wg = s.dWg[l].rearrange("(kc p) n -> p kc n", p=128)
                wu = s.dWu[l].rearrange("(kc p) n -> p kc n", p=128)
                for g in range(11):
                    s.loads.append([((8, 512, 0, 256), wg[:, :, g * 256:(g + 1) * 256]),
                                    ((8, 512, 256, 512), wu[:, :, g * 256:(g + 1) * 256])])
                    bga(("ffn", g))
                wd = s.dWd[l].rearrange("(kc p) n -> p kc n", p=128)
                for g in range(4):
                    s.loads.append([((22, 256, 0, 256), wd[:, :, g * 256:(g + 1) * 256])])
                    bga(("down", g))
        s.nissued = 0
        s.nused = 0
        for i in range(NS):
            s.P.dma_sem(f"w{i}")

    def issue_loads(s, upto):
        while s.nissued < min(upto, len(s.loads)):
            i = s.nissued
            slot = i % NS
            for part, ((kc, n, a, b), src) in enumerate(s.loads[i]):
                dst = s.ws[slot][:, 0:kc * n].rearrange("p (k n) -> p k n", k=kc)[:, :, a:b]
                wk = [("w", slot, 0), ("w", slot, 1)] if len(s.loads[i]) == 1 else [("w", slot, part)]
                s.P.add("pool", lambda e, dst=dst, src=src: e.dma_start(out=dst, in_=src), writes=wk, dma=f"w{slot}")
            s.nissued += 1

    def use_load(s):
        i = s.nused
        s.nused += 1
        s.issue_loads(i + NS)
        slot = i % NS
        return s.ws[slot], [("w", slot, 0), ("w", slot, 1)]

    def setup(s):
        P, L = s.P, s.L
        P.dma_sem("prm")
        P.add("sp", lambda e: e.dma_start(out=s.prm[:], in_=s.dPrm), writes=["prm"], dma="prm")
        P.dma_sem("tabs")
        P.add("sp", lambda e: e.dma_start(out=s.tabs, in_=s.dTabs.rearrange("p (l n) -> p l n", l=L)), reads=["U"], writes=["tabs"], dma="tabs")
        s.op("pool", lambda e: e.memset(s.ones_f[:], 1.0), w=["ones_f"])
        s.op("pool", lambda e: e.affine_select(out=s.ident[:], in_=s.ones_f[:], pattern=[[1, 128]], compare_op=ALU.is_equal,
                                               fill=0.0, base=0, channel_multiplier=-1), r=["ones_f"], w=["ident"])
        s.op("dve", lambda e: e.tensor_copy(out=s.ones_b[:], in_=s.ones_f[:]), r=["ones_f"], w=["ones_b"])
        s.op("dve", lambda e: e.memset(s.blk[:], 0.0), w=["blk"])
        s.op("dve", lambda e: e.memset(s.blk[0:64, 0:64], 1.0 / 64), w=["blk"])
        s.op("dve", lambda e: e.memset(s.blk[64:128, 64:128], 1.0 / 64), w=["blk"])
        s.op("dve", lambda e: e.memset(s.epsc[:], EPS), w=["epsc"])
        s.op("dve", lambda e: e.tensor_tensor(out=s.imb[:], in0=s.ident[:], in1=s.blk[:], op=ALU.subtract), r=["ident", "blk"], w=["imb"])
        for nm in ("cglu", "cpb", "cm", "cG"):
            t = getattr(s, nm)
            s.op("pool", lambda e, t=t: e.memset(t[:], 0.0), w=[nm] if nm != "cG" else [("cG", j) for j in range(22)])
        for j, (wl, wh) in enumerate(((2.0, 4.0), (8.0, 16.0))):
            s.op("dve", lambda e, j=j, wl=wl: e.memset(s.wcol[0:64, j:j + 1], wl), w=["wcol"])
            s.op("dve", lambda e, j=j, wh=wh: e.memset(s.wcol[64:128, j:j + 1], wh), w=["wcol"])
        s.op("dve", lambda e: e.reciprocal(out=s.invw[:], in_=s.wcol[:]), r=["wcol"], w=["invw"])
        for j in range(2):
            s.op("dve", lambda e, j=j: e.tensor_scalar(out=s.rdiv16[:, j, :], in0=s.pc("pos", 0, 16), scalar1=s.wcol[:, j:j + 1],
                                                       scalar2=None, op0=ALU.min), r=["prm", "wcol"], w=[("rdiv", j)])
            s.op("dve", lambda e, j=j: e.reciprocal(out=s.rdiv16[:, j, :], in_=s.rdiv16[:, j, :]), r=[("rdiv", j)], w=[("rdiv", j)])
        s.op("act", lambda e: e.activation(out=s.cact[:], in_=s.pc("c", 0, 8), func=AF.Silu), r=["prm"], w=["cact"])
        s.op("dve", lambda e: e.tensor_copy(out=s.cact_b[:], in_=s.cact[:]), r=["cact"], w=["cact_b"])
        s.op("dve", lambda e: e.tensor_copy(out=s.blk_b[:], in_=s.blk[:]), r=["blk"], w=["blk_b"])
        for l in range(L):
            b = s.bank()
            for j in range(2):
                s.op("pe", lambda e, b=b, j=j, l=l: e.matmul(s.ps[b][:, j:j + 1], lhsT=s.imb[:], rhs=s.pc(f"{l}.cab", j), start=True, stop=True),
                     r=["imb", "prm"], w=[("ps", b)])
            s.op("act", lambda e, b=b, l=l: e.activation(out=s.bcn[:, l, :], in_=s.ps[b][:, 0:2], func=AF.Copy), r=[("ps", b)], w=[("bcn", l)])
            s.op("pool", lambda e, l=l: e.affine_select(out=s.WTm[:, l], in_=s.tabs[:, l, 512:1024].rearrange("p (h t) -> p h t", h=4),
                                                        pattern=[[0, 4], [1, 128]], compare_op=ALU.is_ge, fill=0.0, base=0,
                                                        channel_multiplier=-1), r=["tabs"], w=[("WTm", l)], u=True)
            s.op("dve", lambda e, l=l: e.tensor_copy(out=s.PW[:, l], in_=s.tabs[:, l, 256:512].rearrange("p (j d) -> p j d", j=2)),
                 r=["tabs"], w=[("PW", l)], u=True)
            b = s.bank()
            for h in range(4):
                s.op("pe", lambda e, b=b, h=h, l=l: e.matmul(s.ps[b][:, h * 128:(h + 1) * 128], lhsT=s.ones_b[:], rhs=s.WTm[:, l, h, :],
                                                             start=True, stop=True), r=["ones_b", ("WTm", l)], w=[("ps", b)])
            for j in range(2):
                for hh in range(2):
                    rows = slice(64 * hh, 64 * hh + 64)
                    h = 2 * j + hh
                    s.op("dve", lambda e, b=b, l=l, j=j, h=h, rows=rows: e.scalar_tensor_tensor(
                        out=s.Bfull[rows, l, j, :], in0=s.ps[b][rows, h * 128:(h + 1) * 128],
                        scalar=s.prm[rows, s.off[f"{l}.lnb"] + j: s.off[f"{l}.lnb"] + j + 1],
                        in1=s.tabs[rows, l, j * 128:(j + 1) * 128], op0=ALU.mult, op1=ALU.add),
                        r=[("ps", b), "prm", "tabs"], w=[("Bfull", l, j, hh)], u=True)

    def maybe_ada(s, ti, l, pos):
        if ti == 0 and l + 1 < s.L and pos in ADA_POS:
            s.ada_group(l + 1, ADA_POS.index(pos))

    def ada_group(s, l, g):
        ws, wk = s.use_load()
        wv = ws[:, 0:4096].rearrange("p (k n) -> p k n", k=8)
        b2 = s.bank()
        for q in range(4):
            s.mmg(s.ps[b2][:, q:q + 1], b2, [(wv[:, kc, q * 128:(q + 1) * 128], s.cact_b[:, kc:kc + 1]) for kc in range(8)], wk + ["cact_b"])
        oB = s.off[f"{l}.adab"] + 4 * g
        s.op("dve", lambda e: e.tensor_tensor(out=s.modT[:, l, 4 * g:4 * g + 4], in0=s.ps[b2][:, 0:4], in1=s.prm[:, oB:oB + 4], op=ALU.add),
             r=[("ps", b2), "prm"], w=[("modTg", l, g)])
        if g == 11:
            s.P.add("dve", lambda e: e.tensor_copy(out=s.fz[:, 1:2], in_=s.fz[:, 2:3]), reads=[("modTg", l, q) for q in range(12)], writes=[("modT", l)])
            for wh, (gn, sc0) in enumerate((("n1g", 8), ("n2g", 32))):
                s.op("dve", lambda e, wh=wh, gn=gn, sc0=sc0: e.scalar_tensor_tensor(
                    out=s.gm[:, l, wh, :], in0=s.modT[:, l, sc0:sc0 + 8], scalar=1.0, in1=s.pc(f"{l}.{gn}", 0, 8),
                    op0=ALU.add, op1=ALU.mult), r=[("modT", l), "prm"], w=[("gm", l, wh)])
        s.tk()

    def segs(s, ti):
        W = s.tiles[ti][1]
        n = (W + 511) // 512
        step = ((W // 128 + n - 1) // n) * 128
        out = []
        c = 0
        while c < W:
            out.append((c, min(c + step, W)))
            c += step
        return out

    @staticmethod
    def seg_of(sg, col):
        for i, (c0, c1) in enumerate(sg):
            if c0 <= col < c1:
                return i
        raise ValueError(col)

    def ssb(s, si):
        return 8 - s.nsg + si

    def sumsq(s, kc, si, c0, c1):
        i = s.ri("sqb")
        w = c1 - c0
        cnt = s.ssc.get(si, 0)
        s.ssc[si] = (cnt + 1) % 8
        s.op("act", lambda e: e.activation(out=s.sqb[:, i, 0:w], in_=s.xT[:, kc, c0:c1], func=AF.Square),
             r=[("xT", kc, si)], w=[("sqb", i)])
        bs = s.ssb(si)
        s.op("pe", lambda e: e.matmul(s.ps[bs][:, 0:w], lhsT=s.ones_b[:], rhs=s.sqb[:, i, 0:w], start=(cnt == 0), stop=(cnt == 7)),
             r=[("sqb", i), "ones_b"], w=[("ps", bs)])

    def load_x(s, ti):
        tok0, W = s.tiles[ti]
        sg = s.segs(ti)
        for stt in range(W // 128):
            i = s.ri("stage", 4)
            s.P.add("sp", lambda e, i=i, stt=stt: e.dma_start(out=s.stage[:, i, :], in_=s.dX[tok0 + stt * 128: tok0 + (stt + 1) * 128, :]),
                    reads=["U"], writes=[("stage", i)], dma=f"st{i}")
            si = s.seg_of(sg, stt * 128)
            for half in range(2):
                b = s.bank()
                for q in range(4):
                    kc = half * 4 + q
                    s.op("pe", lambda e, b=b, q=q, kc=kc, i=i: e.transpose(out=s.ps[b][:, q * 128:(q + 1) * 128],
                                                                          in_=s.stage[:, i, kc * 128:(kc + 1) * 128], identity=s.ident[:]),
                         r=[("stage", i), "ident"], w=[("ps", b)], u=True)
                eng = "act" if half == 0 else "dve"
                src = s.ps[b][:].rearrange("p (a t) -> p a t", a=4)
                dst = s.xT[:, half * 4:half * 4 + 4, stt * 128:(stt + 1) * 128]
                wk = [("xT", half * 4 + q, si) for q in range(4)]
                if eng == "act":
                    s.op("act", lambda e, src=src, dst=dst: e.activation(out=dst, in_=src, func=AF.Copy), r=[("ps", b)], w=wk)
                else:
                    s.op("dve", lambda e, src=src, dst=dst: e.tensor_copy(out=dst, in_=src), r=[("ps", b)], w=wk)
        for si, (c0, c1) in enumerate(sg):
            for kc in range(8):
                s.sumsq(kc, si, c0, c1)

    def norm(s, sg, gm_ap, sh_ap, rk, dst_is_x=False):
        s.fence()
        for si, (c0, c1) in enumerate(sg):
            w = c1 - c0
            bs = s.ssb(si)
            s.op("act", lambda e, bs=bs, c0=c0, c1=c1, w=w: e.activation(out=s.rstd[:, c0:c1], in_=s.ps[bs][:, 0:w], func=AF.Sqrt,
                                                                        bias=s.epsc[:, 0:1], scale=1.0 / 1024), r=[("ps", bs), "epsc"], w=[("rstd", si)], u=True)
            s.op("dve", lambda e, c0=c0, c1=c1: e.reciprocal(out=s.rstd[:, c0:c1], in_=s.rstd[:, c0:c1]), r=[("rstd", si)], w=[("rstd", si)], u=True)
            for kc in range(8):
                i = s.ri("ntmp", 4)
                s.op("pool" if kc in (1, 4, 6) else "dve", lambda e, i=i, kc=kc, c0=c0, c1=c1, w=w: e.tensor_tensor(out=s.ntmp[:, i, 0:w], in0=s.xT[:, kc, c0:c1],
                                                                                   in1=s.rstd[:, c0:c1], op=ALU.mult),
                     r=[("xT", kc, si), ("rstd", si)], w=[("ntmp", i)], u=True)
                if dst_is_x:
                    s.op("act", lambda e, i=i, kc=kc, c0=c0, c1=c1, w=w: e.activation(out=s.xT[:, kc, c0:c1], in_=s.ntmp[:, i, 0:w], func=AF.Identity,
                                                                                     scale=gm_ap[:, kc:kc + 1]), r=[("ntmp", i), "prm"], w=[("xT", kc, si)], u=True)
                else:
                    s.op("act", lambda e, i=i, kc=kc, c0=c0, c1=c1, w=w: e.activation(out=s.hb[:, kc, c0:c1], in_=s.ntmp[:, i, 0:w], func=AF.Identity,
                                                                                     scale=gm_ap[:, kc:kc + 1], bias=sh_ap[:, kc:kc + 1]),
                         r=[("ntmp", i)] + rk, w=[("hb", kc, si)], u=True)

    def mix(s, ti, l):
        tok0, W = s.tiles[ti]
        sg = s.segs(ti)
        nsg = len(sg)
        mcol = s.ones_f[:, 0:1]
        hbk = lambda si: [("hb", kc, si) for kc in range(8)]
        s.fence()
        s.op("pool", lambda e: e.tensor_copy(out=s.glu[:, :, 0:30], in_=s.cglu[:, l]), r=["cglu"], w=[("gluh", 0), ("gluh", 1)], u=True)
        s.op("pool", lambda e: e.tensor_copy(out=s.pbx[:, :, 0:16], in_=s.cpb[:, l]), r=["cpb"], w=[("pbxh", 0), ("pbxh", 1)], u=True)
        s.op("pool", lambda e: e.tensor_copy(out=s.m[:, :, 0:2], in_=s.cm[:, l]), r=["cm"], w=[("mh", 0), ("mh", 1)], u=True)
        oD = s.off[f"{l}.cdw"]
        s.op("dve", lambda e: e.tensor_tensor(out=s.diagD, in0=s.ident[:].unsqueeze(1).broadcast_to([128, 6, 128]),
                                              in1=s.prm[:, oD:oD + 6].unsqueeze(2).broadcast_to([128, 6, 128]), op=ALU.mult),
             r=["ident", "prm"], w=["diagD"], u=True)

        def build_diagA(j):
            oA = s.off[f"{l}.caw"] + 31 * j
            s.op("dve", lambda e: e.tensor_tensor(out=s.diagA, in0=s.imb[:].unsqueeze(1).broadcast_to([128, 31, 128]),
                                                  in1=s.prm[:, oA:oA + 31].unsqueeze(2).broadcast_to([128, 31, 128]), op=ALU.mult),
                 r=["imb", "prm"], w=["diagA"], u=True)

        build_diagA(0)

        ws, wk = s.use_load()
        wv = ws[:, 0:4096].rearrange("p (k n) -> p k n", k=8)

        def sgu(si, iv, nst):
            s.sgu_done.add((ti, l, si))
            s.pend[("vn", iv)] = False
            c0, c1 = sg[si]
            w = c1 - c0
            for j in range(2):
                b = s.bank()
                for stt in range(nst):
                    for hh in range(2):
                        h = 2 * j + hh
                        s.op("pe", lambda e, b=b, stt=stt, hh=hh, h=h: e.matmul(
                            s.ps[b][64 * hh:64 * hh + 64, stt * 128:(stt + 1) * 128], lhsT=s.vn[:, iv, stt, h * 64:(h + 1) * 64],
                            rhs=s.WTm[:, l, h, :], start=True, stop=True), r=[("vn", iv), ("WTm", l)], w=[("ps", b)], u=True)
                s.op("dve", lambda e, b=b, j=j: e.scalar_tensor_tensor(
                    out=s.svb[:, j, c0:c1].rearrange("p (a t) -> p a t", a=nst), in0=s.ps[b][:, 0:w].rearrange("p (a t) -> p a t", a=nst),
                    scalar=s.pc(f"{l}.lng", j), in1=s.Bfull[:, l, j, :].unsqueeze(1).broadcast_to([128, nst, 128]),
                    op0=ALU.mult, op1=ALU.add), r=[("ps", b), "prm", ("Bfull", l, j, 0), ("Bfull", l, j, 1)], w=[("svb", j, si)], u=True)
            s.tk()

        for si, (c0, c1) in enumerate(sg):
            w = c1 - c0
            nst = w // 128
            iv = s.ri_wait("vn")
            groups = []
            vbanks = []
            for p0 in range(0, nst):
                b = s.bank()
                npair = 1
                vbanks.append((b, p0, npair))
                stt = p0
                groups.append((s.ps[b][:, 0:256], b,
                               [(s.hb[:, kc, c0 + stt * 128:c0 + (stt + 1) * 128], wv[:, kc, 0:256], wk + [("hb", kc, si)]) for kc in range(8)]))
            bgb = []
            for j in range(2):
                b = s.bank()
                bgb.append(b)
                groups.append((s.ps[b][:, 0:w], b, [(wv[:, kc, 256 + j * 128:256 + (j + 1) * 128], s.hb[:, kc, c0:c1], wk + [("hb", kc, si)]) for kc in range(8)]))
            s.mmi(groups)
            for (b, p0, npair) in vbanks:
                s.op("act", lambda e, b=b, p0=p0, npair=npair: e.activation(
                    out=s.vsb[:, p0:p0 + npair, :], in_=s.ps[b][:, 0:npair * 256].rearrange("p (a c) -> p a c", a=npair), func=AF.Copy),
                    r=[("ps", b)], w=[("vsb", p0)], u=True)
            for j in range(2):
                s.op("act", lambda e, b=bgb[j], j=j, c0=c0, c1=c1, w=w: e.activation(out=s.bgs[:, j, c0:c1], in_=s.ps[b][:, 0:w], func=AF.Copy),
                     r=[("ps", b)], w=[("bgs", j, si)], u=True)
            vk = [("vsb", p0) for p0 in range(0, nst)]
            vsbV, vtmpV, vnV = s.vsb[:, 0:nst, :], s.vtmp[:, 0:nst, :], s.vn[:, iv, 0:nst, :]
            S_ = [s.vst[:, q, 0:nst] for q in range(6)]
            bmean = S_[2].unsqueeze(2).broadcast_to([128, nst, 256])
            brstd = S_[5].unsqueeze(2).broadcast_to([128, nst, 256])
            s.op("dve", lambda e, vsbV=vsbV, vtmpV=vtmpV: e.tensor_tensor(out=vtmpV, in0=vsbV, in1=vsbV, op=ALU.mult), r=vk, w=["vtmp"], u=True)
            s.op("dve", lambda e, vsbV=vsbV, S_=S_: e.tensor_reduce(out=S_[0], in_=vsbV, axis=AX.X, op=ALU.add), r=vk, w=[("vst", 0)], u=True)
            s.op("dve", lambda e, vtmpV=vtmpV, S_=S_: e.tensor_reduce(out=S_[1], in_=vtmpV, axis=AX.X, op=ALU.add), r=["vtmp"], w=[("vst", 1)], u=True)
            s.op("dve", lambda e, S_=S_: e.tensor_scalar(out=S_[2], in0=S_[0], scalar1=1.0 / 256, scalar2=None, op0=ALU.mult),
                 r=[("vst", 0)], w=[("vst", 2)])
            s.op("dve", lambda e, S_=S_: e.tensor_tensor(out=S_[3], in0=S_[2], in1=S_[2], op=ALU.mult), r=[("vst", 2)], w=[("vst", 3)])
            s.op("dve", lambda e, S_=S_: e.scalar_tensor_tensor(out=S_[4], in0=S_[1], scalar=1.0 / 256, in1=S_[3],
                                                         op0=ALU.mult, op1=ALU.subtract), r=[("vst", 1), ("vst", 3)], w=[("vst", 4)])
            s.op("act", lambda e, S_=S_: e.activation(out=S_[5], in_=S_[4], func=AF.Sqrt, bias=s.epsc[:, 0:1], scale=1.0),
                 r=[("vst", 4), "epsc"], w=[("vst", 5)])
            s.op("dve", lambda e, S_=S_: e.reciprocal(out=S_[5], in_=S_[5]), r=[("vst", 5)], w=[("vst", 5)])
            s.op("dve", lambda e, vsbV=vsbV, vtmpV=vtmpV, bmean=bmean: e.tensor_tensor(out=vtmpV, in0=vsbV, in1=bmean, op=ALU.subtract),
                 r=vk + [("vst", 2), "vtmp"], w=["vtmp"], u=True)
            s.op("dve", lambda e, vtmpV=vtmpV, vnV=vnV, brstd=brstd: e.tensor_tensor(out=vnV, in0=vtmpV, in1=brstd, op=ALU.mult),
                 r=["vtmp", ("vst", 5)], w=[("vn", iv)], u=True)
            s.defer(10, lambda si=si, iv=iv, nst=nst: sgu(si, iv, nst))
            s.tk(4)
        s.maybe_ada(ti, l, ("mix", 0))
        ws, wk = s.use_load()
        wv = ws[:, 0:4096].rearrange("p (k n) -> p k n", k=8)

        def var_A(j, si, i):
            c0, c1 = sg[si]
            w = c1 - c0
            b = s.bank()
            ir = s.ri("rsA")
            s.op("pe", lambda e: e.matmul(s.ps[b][:, 0:w], lhsT=s.blk_b[:], rhs=s.sqA[:, i, 0:w], start=True, stop=True),
                 r=[("sqA", i), "blk_b"], w=[("ps", b)], u=True)
            s.op("act", lambda e: e.activation(out=s.rsA[:, ir, 0:w], in_=s.ps[b][:, 0:w], func=AF.Sqrt, bias=s.epsc[:, 0:1], scale=1.0),
                 r=[("ps", b), "epsc"], w=[("rsA", ir)], u=True)
            s.op("dve", lambda e: e.reciprocal(out=s.rsA[:, ir, 0:w], in_=s.rsA[:, ir, 0:w]), r=[("rsA", ir)], w=[("rsA", ir)], u=True)
            s.op("dve", lambda e: e.tensor_tensor(out=s.cA[:, i, 0:w], in0=s.cA[:, i, 0:w], in1=s.rsA[:, ir, 0:w], op=ALU.mult),
                 r=[("cA", i), ("rsA", ir)], w=[("cA", i)], u=True)
            def fin():
                s.op("act", lambda e: e.activation(out=s.y[:, j, c0:c1], in_=s.cA[:, i, 0:w], func=AF.Silu,
                                                   scale=s.pc(f"{l}.gng", j), bias=s.pc(f"{l}.gnb", j)),
                     r=[("cA", i), "prm"], w=[("y", j, si)], u=True)
                s.pend[("Aunit", i)] = False
            s.defer(2, fin)
            s.tk()

        def conv_A(j, si, i):
            c0, c1 = sg[si]
            w = c1 - c0
            b = s.bank()
            rk = [("glu", j, si), ("glu", j, si - 1) if si > 0 else ("gluh", j), "diagA"]
            s.mmg(s.ps[b][:, 0:w], b, [(s.diagA[:, k, :], s.glu[:, j, c0 + k:c0 + k + w]) for k in range(31)], rk, u=True)
            s.op("act", lambda e: e.activation(out=s.cA[:, i, 0:w], in_=s.ps[b][:, 0:w], func=AF.Identity, bias=s.bcn[:, l, j:j + 1], scale=1.0),
                 r=[("ps", b), ("bcn", l)], w=[("cA", i)], u=True)
            s.op("act", lambda e: e.activation(out=s.sqA[:, i, 0:w], in_=s.ps[b][:, 0:w], func=AF.Square, bias=s.bcn[:, l, j:j + 1], scale=1.0),
                 r=[("ps", b), ("bcn", l)], w=[("sqA", i)], u=True)
            if j == 0 and si == nsg - 1:
                build_diagA(1)
                st8["rebuilt"] = True
                for k, (si2, i2) in enumerate(st8["wait"]):
                    s.defer(5 + 3 * k, lambda si2=si2, i2=i2: conv_A(1, si2, i2))
                st8["wait"] = []
            s.tk(3)
            s.defer(3, lambda: var_A(j, si, i))

        st8 = {"rebuilt": False, "wait": []}
        for si, (c0, c1) in enumerate(sg):
            for j in range(2):
                w = c1 - c0
                i = s.ri_wait("Aunit", ADEPTH)
                isg = s.ri("sig")
                ba = s.bank()
                s.mmg(s.ps[ba][:, 0:w], ba, [(wv[:, kc, j * 128:(j + 1) * 128], s.hb[:, kc, c0:c1]) for kc in range(8)], wk + hbk(si))
                bg = s.bank()
                s.mmg(s.ps[bg][:, 0:w], bg, [(wv[:, kc, (2 + j) * 128:(3 + j) * 128], s.hb[:, kc, c0:c1]) for kc in range(8)], wk + hbk(si))
                s.op("act", lambda e, isg=isg, bg=bg, w=w: e.activation(out=s.sig[:, isg, 0:w], in_=s.ps[bg][:, 0:w], func=AF.Sigmoid),
                     r=[("ps", bg)], w=[("sig", isg)], u=True)
                s.op("dve", lambda e, isg=isg, ba=ba, j=j, c0=c0, c1=c1, w=w: e.tensor_tensor(out=s.glu[:, j, 30 + c0:30 + c1], in0=s.ps[ba][:, 0:w],
                                                                                            in1=s.sig[:, isg, 0:w], op=ALU.mult),
                     r=[("ps", ba), ("sig", isg)], w=[("glu", j, si)], u=True)
                if ti == 0 and si == 0:
                    s.op("pool", lambda e, j=j, c0=c0, c1=c1: e.tensor_scalar(out=s.glu[:, j, 30:30 + HALO], in0=s.glu[:, j, 30:30 + HALO],
                                                                              scalar1=s.pc("mask"), scalar2=None, op0=ALU.mult),
                         r=[("glu", j, si), "prm"], w=[("glu", j, si)], u=True)
                if j == 0 or st8["rebuilt"]:
                    s.defer(3, lambda j=j, si=si, i=i: conv_A(j, si, i))
                else:
                    st8["wait"].append((si, i))
                s.tk(2)

        s.maybe_ada(ti, l, ("mix", 1))
        ws, wk = s.use_load()
        wv = ws[:, 0:4096].rearrange("p (k n) -> p k n", k=8)

        def poolmm(j, si, ip):
            c0, c1 = sg[si]
            w = c1 - c0
            b = s.bank()
            s.op("pe", lambda e: e.matmul(s.ps[b][:, 0:w], lhsT=s.PW[:, l, j, :], rhs=s.pooled[:, ip, 0:w], start=True, stop=True),
                 r=[("pooled", ip, 0), ("pooled", ip, 1), ("PW", l)], w=[("ps", b)], u=True)
            s.op("act", lambda e: e.activation(out=s.y[:, 2 + j, c0:c1], in_=s.ps[b][:, 0:w], func=AF.Identity, scale=s.pc(f"{l}.psc", j)),
                 r=[("ps", b), "prm"], w=[("y", 2 + j, si)], u=True)
            s.pend[("pooled", ip)] = False
            s.tk()

        jobs = []

        def pool_job(k):
            j, si, c0, c1 = jobs[k]
            w = c1 - c0
            o = 16 + c0
            xk = [("pbx", j, si), ("pbx", j, si - 1) if si > 0 else ("pbxh", j)]
            x = s.pbx[:, j, :]
            s.op("pool", lambda e: e.tensor_tensor(out=s.sA[:, 2:16 + w], in0=x[:, o - 14:o + w], in1=x[:, o - 15:o + w - 1], op=ALU.add),
                 r=xk, w=["sA"], u=True)
            if j == 0:
                s.op("pool", lambda e: e.tensor_tensor(out=s.sB[64:128, 4:16 + w], in0=s.sA[64:128, 4:16 + w], in1=s.sA[64:128, 2:14 + w], op=ALU.add),
                     r=["sA"], w=["sB"], u=True)
            else:
                s.op("pool", lambda e: e.tensor_tensor(out=s.sB[:, 4:16 + w], in0=s.sA[:, 4:16 + w], in1=s.sA[:, 2:14 + w], op=ALU.add),
                     r=["sA"], w=["sB"], u=True)
                s.op("pool", lambda e: e.tensor_tensor(out=s.sA[:, 8:16 + w], in0=s.sB[:, 8:16 + w], in1=s.sB[:, 4:12 + w], op=ALU.add),
                     r=["sB", "sA"], w=["sA"], u=True)
                s.op("pool", lambda e: e.tensor_tensor(out=s.sB[64:128, 16:16 + w], in0=s.sA[64:128, 16:16 + w], in1=s.sA[64:128, 8:8 + w], op=ALU.add),
                     r=["sA", "sB"], w=["sB"], u=True)
            s.defer(4, lambda: fin_job(k))

        def fin_job(k):
            j, si, c0, c1 = jobs[k]
            w = c1 - c0
            o = 16 + c0
            xk = [("pbx", j, si), ("pbx", j, si - 1) if si > 0 else ("pbxh", j)]
            ip = s.ri_wait("pooled", 4)
            for hh, src in ((0, s.sA), (1, s.sB)):
                rows = slice(64 * hh, 64 * hh + 64)
                s.op("dve", lambda e, rows=rows, src=src: e.scalar_tensor_tensor(
                    out=s.pooled[rows, ip, 0:w], in0=src[rows, 16:16 + w], scalar=s.invw[rows, j:j + 1], in1=s.pbx[rows, j, o:o + w],
                    op0=ALU.mult, op1=ALU.subtract), r=["sA", "sB", "invw"] + xk, w=[("pooled", ip, hh)], u=True)
                if ti == 0 and c0 <= HALO < c1:
                    fo = HALO - c0
                    s.op("dve", lambda e, rows=rows, src=src, fo=fo: e.tensor_tensor(out=s.tmp16[rows, :], in0=src[rows, 16 + fo:32 + fo], in1=s.rdiv16[rows, j, :], op=ALU.mult),
                         r=["sA", "sB", ("rdiv", j)], w=[("tmp16", hh)], u=True)
                    s.op("dve", lambda e, rows=rows, fo=fo: e.tensor_tensor(out=s.pooled[rows, ip, fo:fo + 16], in0=s.tmp16[rows, :], in1=s.pbx[rows, j, o + fo:o + fo + 16], op=ALU.subtract),
                         r=[("tmp16", hh), ("pooled", ip, hh)] + xk, w=[("pooled", ip, hh)], u=True)
            s.defer(2, lambda: poolmm(j, si, ip))
            if k + 1 < len(jobs):
                pool_job(k + 1)
            else:
                st8["prun"] = False

        for j in range(2):
            for si, (c0, c1) in enumerate(sg):
                w = c1 - c0
                b = s.bank()
                s.mmg(s.ps[b][:, 0:w], b, [(wv[:, kc, j * 128:(j + 1) * 128], s.hb[:, kc, c0:c1]) for kc in range(8)], wk + hbk(si))
                s.op("act", lambda e, b=b, j=j, c0=c0, c1=c1, w=w: e.activation(out=s.pbx[:, j, 16 + c0:16 + c1], in_=s.ps[b][:, 0:w], func=AF.Copy),
                     r=[("ps", b)], w=[("pbx", j, si)], u=True)
                if ti == 0 and si == 0:
                    s.op("pool", lambda e, j=j, c0=c0, c1=c1: e.tensor_scalar(out=s.pbx[:, j, 16:16 + HALO], in0=s.pbx[:, j, 16:16 + HALO],
                                                                              scalar1=s.pc("mask"), scalar2=None, op0=ALU.mult),
                         r=[("pbx", j, si), "prm"], w=[("pbx", j, si)], u=True)
                jobs.append((j, si, c0, c1))
                if not st8.get("prun"):
                    st8["prun"] = True
                    s.defer(1, lambda k=len(jobs) - 1: pool_job(k))
                s.tk()
        for j in range(2):
            for si, (c0, c1) in enumerate(sg):
                w = c1 - c0
                b = s.bank()
                s.mmg(s.ps[b][:, 0:w], b, [(wv[:, kc, 256 + j * 128:256 + (j + 1) * 128], s.hb[:, kc, c0:c1]) for kc in range(8)], wk + hbk(si))
                if (ti, l, si) not in s.sgu_done:
                    s.flush()
                s.op("dve", lambda e, b=b, j=j, c0=c0, c1=c1, w=w: e.tensor_tensor(out=s.y[:, 4 + j, c0:c1], in0=s.ps[b][:, 0:w], in1=s.svb[:, j, c0:c1], op=ALU.mult),
                     r=[("ps", b), ("svb", j, si)], w=[("y", 4 + j, si)], u=True)
                s.tk()

        s.maybe_ada(ti, l, ("mix", 2))
        ws, wk = s.use_load()
        wv = ws[:, 0:4096].rearrange("p (k n) -> p k n", k=8)

        def conv_D(j, si):
            c0, c1 = sg[si]
            w = c1 - c0
            b = s.bank()
            rk = [("m", j, si), ("m", j, si - 1) if si > 0 else ("mh", j), "diagD"]
            s.mmg(s.ps[b][:, 0:w], b, [(s.diagD[:, j * 3 + k, :], s.m[:, j, c0 + k:c0 + k + w]) for k in range(3)], rk, u=True)
            s.op("dve", lambda e: e.tensor_tensor(out=s.y[:, 6 + j, c0:c1], in0=s.ps[b][:, 0:w], in1=s.bgs[:, j, c0:c1], op=ALU.mult),
                 r=[("ps", b), ("bgs", j, si)], w=[("y", 6 + j, si)], u=True)
            s.tk()

        for j in range(2):
            for si, (c0, c1) in enumerate(sg):
                w = c1 - c0
                i = s.ri("hhs")
                bc_ = s.bank()
                s.mmg(s.ps[bc_][:, 0:w], bc_, [(wv[:, kc, j * 128:(j + 1) * 128], s.hb[:, kc, c0:c1]) for kc in range(8)], wk + hbk(si))
                bh = s.bank()
                s.mmg(s.ps[bh][:, 0:w], bh, [(wv[:, kc, 256 + j * 128:256 + (j + 1) * 128], s.hb[:, kc, c0:c1]) for kc in range(8)], wk + hbk(si))
                s.op("act", lambda e, i=i, bh=bh, w=w: e.activation(out=s.hhs[:, i, 0:w], in_=s.ps[bh][:, 0:w], func=AF.Copy),
                     r=[("ps", bh)], w=[("hhs", i)], u=True)
                s.op("dve", lambda e, i=i, bc_=bc_, j=j, c0=c0, c1=c1, w=w: e.tensor_tensor(out=s.m[:, j, 2 + c0:2 + c1], in0=s.ps[bc_][:, 0:w], in1=s.hhs[:, i, 0:w], op=ALU.mult),
                     r=[("ps", bc_), ("hhs", i)], w=[("m", j, si)], u=True)
                if ti == 0 and si == 0:
                    s.op("pool", lambda e, j=j, c0=c0, c1=c1: e.tensor_scalar(out=s.m[:, j, 2:2 + HALO], in0=s.m[:, j, 2:2 + HALO],
                                                                              scalar1=s.pc("mask"), scalar2=None, op0=ALU.mult),
                         r=[("m", j, si), "prm"], w=[("m", j, si)], u=True)
                s.defer(3, lambda j=j, si=si: conv_D(j, si))
                s.tk(2)
        s.maybe_ada(ti, l, ("mix", 3))
        s.flush()
        last = nsg - 1
        s.op("pool", lambda e: e.tensor_scalar(out=s.cglu[:, l], in0=s.glu[:, :, W:W + 30], scalar1=mcol, scalar2=None, op0=ALU.mult),
             r=[("glu", 0, last), ("glu", 1, last), "prm"], w=["cglu"], u=True)
        s.op("pool", lambda e: e.tensor_scalar(out=s.cpb[:, l], in0=s.pbx[:, :, W:W + 16], scalar1=mcol, scalar2=None, op0=ALU.mult),
             r=[("pbx", 0, last), ("pbx", 1, last), "prm"], w=["cpb"], u=True)
        s.op("pool", lambda e: e.tensor_scalar(out=s.cm[:, l], in0=s.m[:, :, W:W + 2], scalar1=mcol, scalar2=None, op0=ALU.mult),
             r=[("m", 0, last), ("m", 1, last), "prm"], w=["cm"], u=True)

    def wout(s, ti, l):
        tok0, W = s.tiles[ti]
        sg = s.segs(ti)
        for g in range(2):
            ws, wk = s.use_load()
            wv = ws[:, 0:4096].rearrange("p (k n) -> p k n", k=8)
            for mm in range(4):
                m = 4 * g + mm
                for si, (c0, c1) in enumerate(sg):
                    w = c1 - c0
                    b = s.bank()
                    s.mmg(s.ps[b][:, 0:w], b, [(wv[:, kc, mm * 128:(mm + 1) * 128], s.y[:, kc, c0:c1]) for kc in range(8)],
                          wk + [("y", kc, si) for kc in range(8)], u=True)
                    s.op("dve", lambda e, b=b, m=m, c0=c0, c1=c1, w=w: e.scalar_tensor_tensor(
                        out=s.xT[:, m, c0:c1], in0=s.ps[b][:, 0:w], scalar=s.modT[:, l, 16 + m:17 + m], in1=s.xT[:, m, c0:c1],
                        op0=ALU.mult, op1=ALU.add), r=[("ps", b), ("modT", l), ("xT", m, si)], w=[("xT", m, si)])
                    s.defer(4, lambda m=m, si=si, c0=c0, c1=c1: s.sumsq(m, si, c0, c1))
                    s.tk()
            s.maybe_ada(ti, l, ("wout", g))
        s.flush()

    def ffn(s, ti, l):
        tok0, W = s.tiles[ti]
        sg = s.segs(ti)
        nsg = len(sg)
        mcol = s.ones_f[:, 0:1]
        oF = s.off[f"{l}.fcw"]
        s.fence()
        tail = [None]
        for g in range(11):
            ws, wk = s.use_load()
            wv = ws[:, 0:4096].rearrange("p (k n) -> p k n", k=8)
            ii = [s.ri("G"), s.ri("G")]
            for jj in range(2):
                j = 2 * g + jj
                s.op("act", lambda e, i=ii[jj], j=j: e.activation(out=s.G[:, i, 0:2], in_=s.cG[:, l, j, :], func=AF.Copy),
                     r=[("cG", j)], w=[("Gh", ii[jj])], u=True)
            def grp(jj, bank_, c0, c1, si, up):
                co = (256 if up else 0) + jj * 128
                return (s.ps[bank_][:, 0:c1 - c0], bank_, [(wv[:, kc, co:co + 128], s.hb[:, kc, c0:c1], wk + [("hb", kc, si)]) for kc in range(8)])

            def consume(jj, si, c0, c1, w, bg, bu):
                j = 2 * g + jj
                i = ii[jj]
                i2 = s.ri("Funit", 4)
                gk = [("G", i, si), ("G", i, si - 1) if si > 0 else ("Gh", i)]
                s.op("act", lambda e, i=i, bg=bg, c0=c0, c1=c1, w=w: e.activation(out=s.G[:, i, 2 + c0:2 + c1], in_=s.ps[bg][:, 0:w], func=AF.Copy),
                     r=[("ps", bg)], w=[("G", i, si)], u=True)
                if ti == 0 and si == 0:
                    s.op("pool", lambda e, i=i: e.tensor_scalar(out=s.G[:, i, 2:2 + HALO], in0=s.G[:, i, 2:2 + HALO],
                                                                scalar1=s.pc("mask"), scalar2=None, op0=ALU.mult),
                         r=[("G", i, si), "prm"], w=[("G", i, si)], u=True)
                s.op("act", lambda e, i2=i2, bg=bg, j=j, w=w: e.activation(out=s.G2[:, i2, 0:w], in_=s.ps[bg][:, 0:w], func=AF.Identity,
                                                                         scale=s.prm[:, oF + 44 + j:oF + 45 + j]),
                     r=[("ps", bg), "prm"], w=[("G2", i2)], u=True)
                if si == nsg - 1:
                    s.op("act", lambda e, bg=bg, j=j, w=w: e.activation(out=s.cG[:, l, j, :], in_=s.ps[bg][:, w - 2:w], func=AF.Identity, scale=mcol),
                         r=[("ps", bg), "prm"], w=[("cG", j)])
                s.op("dve", lambda e, i=i, i2=i2, j=j, c0=c0, c1=c1, w=w: e.scalar_tensor_tensor(
                    out=s.ft[:, i2, 0:w], in0=s.G[:, i, 1 + c0:1 + c1], scalar=s.prm[:, oF + 22 + j:oF + 23 + j], in1=s.G2[:, i2, 0:w],
                    op0=ALU.mult, op1=ALU.add), r=gk + [("G2", i2), "prm"], w=[("ft", i2)], u=True)
                s.op("dve", lambda e, i=i, i2=i2, j=j, c0=c0, c1=c1, w=w: e.scalar_tensor_tensor(
                    out=s.ft[:, i2, 0:w], in0=s.G[:, i, c0:c1], scalar=s.prm[:, oF + j:oF + j + 1], in1=s.ft[:, i2, 0:w],
                    op0=ALU.mult, op1=ALU.add), r=gk + [("ft", i2), "prm"], w=[("ft", i2)], u=True)
                if tail[0] is not None:
                    tail[0]()

                def mk_tail(i2=i2, bu=bu, j=j, c0=c0, c1=c1, w=w, si=si):
                    def t():
                        s.op("act", lambda e: e.activation(out=s.fs[:, i2, 0:w], in_=s.ft[:, i2, 0:w], func=AF.Silu),
                             r=[("ft", i2)], w=[("fs", i2)], u=True)
                        s.op("dve", lambda e: e.tensor_tensor(out=s.a[:, j, c0:c1], in0=s.ps[bu][:, 0:w], in1=s.fs[:, i2, 0:w], op=ALU.mult),
                             r=[("ps", bu), ("fs", i2)], w=[("a", j, si)], u=True)
                    return t
                tail[0] = mk_tail()

            for si, (c0, c1) in enumerate(sg):
                w = c1 - c0
                if g == 0:
                    bks = [(s.bank(), s.bank()) for _ in range(2)]
                    s.mmi([grp(jj, bks[jj][u_], c0, c1, si, bool(u_)) for jj in range(2) for u_ in range(2)])
                    for jj in range(2):
                        consume(jj, si, c0, c1, w, *bks[jj])
                    s.tk(4)
                else:
                    for jj in range(2):
                        bg, bu = s.bank(), s.bank()
                        s.mmi([grp(jj, bg, c0, c1, si, False), grp(jj, bu, c0, c1, si, True)])
                        consume(jj, si, c0, c1, w, bg, bu)
                        s.tk(2)
            s.maybe_ada(ti, l, ("ffn", g))
        if tail[0] is not None:
            tail[0]()

    def down(s, ti, l):
        tok0, W = s.tiles[ti]
        sg = s.segs(ti)
        for g in range(4):
            ws, wk = s.use_load()
            wv = ws[:, 0:5632].rearrange("p (k n) -> p k n", k=22)
            for mm in range(2):
                m = 2 * g + mm
                for si, (c0, c1) in enumerate(sg):
                    w = c1 - c0
                    b = s.bank()
                    s.mmg(s.ps[b][:, 0:w], b, [(wv[:, kc, mm * 128:(mm + 1) * 128], s.a[:, kc, c0:c1]) for kc in range(22)],
                          wk + [("a", kc, si) for kc in range(22)], u=True)
                    s.op("dve", lambda e, b=b, m=m, c0=c0, c1=c1, w=w: e.scalar_tensor_tensor(
                        out=s.xT[:, m, c0:c1], in0=s.ps[b][:, 0:w], scalar=s.modT[:, l, 40 + m:41 + m], in1=s.xT[:, m, c0:c1],
                        op0=ALU.mult, op1=ALU.add), r=[("ps", b), ("modT", l), ("xT", m, si)], w=[("xT", m, si)])
                    s.defer(4, lambda m=m, si=si, c0=c0, c1=c1: s.sumsq(m, si, c0, c1))
                    s.tk(3)
            s.maybe_ada(ti, l, ("down", g))
        s.flush()

    def store(s, ti):
        tok0, W = s.tiles[ti]
        sg = s.segs(ti)
        for stt in range(1 if ti == 0 else 0, W // 128):
            i = s.ri("stage", 4)
            si = s.seg_of(sg, stt * 128)
            for half in range(2):
                b = s.bank()
                for q in range(4):
                    kc = half * 4 + q
                    s.op("pe", lambda e, b=b, q=q, kc=kc, stt=stt: e.transpose(out=s.ps[b][:, q * 128:(q + 1) * 128],
                                                                            in_=s.xT[:, kc, stt * 128:(stt + 1) * 128], identity=s.ident[:]),
                         r=[("xT", kc, si), "ident"], w=[("ps", b)])
                if half == 0:
                    s.op("act", lambda e, b=b, i=i: e.activation(out=s.stage[:, i, 0:512], in_=s.ps[b][:], func=AF.Copy),
                         r=[("ps", b)], w=[("stage", i)], u=True)
                else:
                    s.op("dve", lambda e, b=b, i=i: e.tensor_copy(out=s.stage[:, i, 512:1024], in_=s.ps[b][:]),
                         r=[("ps", b), ("stage", i)], w=[("stage", i)], u=True)
            r0 = tok0 - HALO + stt * 128
            s.P.add("sp", lambda e, i=i, r0=r0: e.dma_start(out=s.dOut[r0:r0 + 128, :], in_=s.stage[:, i, :]),
                    reads=[("stage", i), "U"], writes=[("out", i)], dma=f"st{i}")

    def run(s):
        L = s.L
        s.plan_loads()
        for i in range(4):
            s.P.dma_sem(f"st{i}")
        s.P.phase = "setup"
        s.setup()
        for ti in range(len(s.tiles)):
            tok0, W = s.tiles[ti]
            sg = s.segs(ti)
            s.nsg = len(sg)
            s.nrot = 8 - s.nsg
            s.P.phase = f"t{ti}.load"
            s.fence()
            s.load_x(ti)
            if ti == 0:
                s.P.phase = "ada0"
                for g in range(12):
                    s.ada_group(0, g)
            for l in range(L):
                s.P.phase = f"t{ti}.l{l}.norm1"
                s.norm(sg, s.gm[:, l, 0, :], s.modT[:, l, 0:8], [("gm", l, 0), ("modT", l)])
                s.P.phase = f"t{ti}.l{l}.mix"
                s.nrot = 8
                s.mix(ti, l)
                s.nrot = 8 - s.nsg
                s.P.phase = f"t{ti}.l{l}.wout"
                s.wout(ti, l)
                s.P.phase = f"t{ti}.l{l}.norm2"
                s.norm(sg, s.gm[:, l, 1, :], s.modT[:, l, 24:32], [("gm", l, 1), ("modT", l)])
                s.P.phase = f"t{ti}.l{l}.ffn"
                s.nrot = 8
                s.ffn(ti, l)
                s.nrot = 8 - s.nsg
                s.P.phase = f"t{ti}.l{l}.down"
                s.down(ti, l)
            s.P.phase = f"t{ti}.final"
            if True:
                if s.final:
                    s.norm(sg, s.pc("fg", 0, 8), None, ["prm"], dst_is_x=True)
                s.fence()
                s.store(ti)
            else:
                pass
        s.P.add("sp", None, reads=[("out", i) for i in range(4)])
        return s.P.emit()


def build_nc(L, ntok, TW, final=True):
    nc = bass.Bass("TRN2", target_bir_lowering=False)
    with ExitStack() as st:
        st.enter_context(nc.allow_low_precision("bf16 matmul operands, fp32 accumulation"))
        k = Kern(nc, st, L, ntok, TW, final)
        info = k.run()
    info["prog"] = k.P
    return nc, info


def pack_inputs(inp, L, S, ncores, final=True):
    B = inp["x"].shape[0]
    halves = ncores // B
    ntok = S // halves
    off, NPRM = prm_layout(L)
    f = lambda a: np.asarray(a, dtype=np.float32)
    tabs = np.zeros((128, L, 1024), np.float32)
    base = np.zeros((128, NPRM), np.float32)
    for l in range(L):
        def put(nm, arr):
            base[:, off[f"{l}.{nm}"]: off[f"{l}.{nm}"] + arr.shape[1]] = arr
        put("n1g", vec_cols(f(inp["norm1_g"])[l]))
        put("n2g", vec_cols(f(inp["norm2_g"])[l]))
        put("cab", vec_cols(f(inp["conv_a_b"])[l]))
        put("adab", vec_cols(f(inp["ada_b"])[l]))
        put("gng", vec_cols(f(inp["gn_a_g"])[l]))
        put("gnb", vec_cols(f(inp["gn_a_b"])[l]))
        put("psc", vec_cols(f(inp["pool_scale"])[l]))
        put("lng", vec_cols(f(inp["sgu_ln_g"])[l]))
        put("lnb", vec_cols(f(inp["sgu_ln_b"])[l]))
        caw = f(inp["conv_a_w"])[l]
        put("caw", np.ascontiguousarray(caw.T.reshape(2, 128, 31).transpose(1, 0, 2).reshape(128, 62)))
        cdw = f(inp["conv_d_w"])[l]
        put("cdw", np.ascontiguousarray(cdw.T.reshape(2, 128, 3).transpose(1, 0, 2).reshape(128, 6)))
        fcw = f(inp["ffn_conv_w"])[l]
        put("fcw", np.ascontiguousarray(fcw.reshape(3, 22, 128).transpose(2, 0, 1).reshape(128, 66)))
        sb = f(inp["sgu_b"])[l]
        for j in range(2):
            for hh in range(2):
                tabs[64 * hh:64 * hh + 64, l, j * 128:(j + 1) * 128] = sb[2 * j + hh][None, :]
        pw = f(inp["pool_w"])[l]
        for j in range(2):
            for hh in range(2):
                tabs[64 * hh:64 * hh + 64, l, 256 + j * 128 + 64 * hh: 256 + j * 128 + 64 * hh + 64] = pw[2 * j + hh]
        sw = f(inp["sgu_w"])[l]
        tabs[:, l, 512:1024] = sw.transpose(2, 0, 1).reshape(128, 512)
    base[:, off["fg"]:off["fg"] + 8] = vec_cols(f(inp["final_g"]))
    maps = []
    x = f(inp["x"])
    shared = dict(tabs=tabs.reshape(128, L * 1024), ada_w=f(inp["ada_w"])[:L], w_in=f(inp["w_in"])[:L],
                  w_out=f(inp["w_out"])[:L], ffn_w_gate=f(inp["ffn_w_gate"])[:L], ffn_w_up=f(inp["ffn_w_up"])[:L],
                  ffn_w_down=f(inp["ffn_w_down"])[:L])
    for c in range(ncores):
        b, hf = divmod(c, halves)
        t0 = hf * ntok
        xs = np.zeros((HALO + ntok, 1024), np.float32)
        xs[HALO:] = x[b, t0:t0 + ntok]
        if hf > 0:
            xs[:HALO] = x[b, t0 - HALO:t0]
        prm = base.copy()
        prm[:, off["c"]:off["c"] + 8] = vec_cols(f(inp["c"])[b])
        prm[:, off["mask"]] = 1.0 if hf > 0 else 0.0
        prm[:, off["pos"]:off["pos"] + 16] = (t0 + 1 + np.arange(16, dtype=np.float32))[None, :]
        d = dict(shared)
        d["xs"] = xs
        d["prm"] = prm
        maps.append(d)
    return maps, ntok, halves


_CACHE = {}


def run_model(inp, L, TW=1024, final=True, trace=False):
    B, S, _ = inp["x"].shape
    ncores = 8
    maps, ntok, halves = pack_inputs(inp, L, S, ncores, final)
    key = (L, ntok, TW, final)
    if key not in _CACHE:
        _CACHE[key] = build_nc(L, ntok, TW, final)
    nc, info = _CACHE[key]
    res = run_bass_kernel_spmd(nc, maps, core_ids=list(range(ncores)), **({"trace": True} if trace else {}))
    out = np.zeros((B, S, 1024), np.float32)
    for c in range(ncores):
        b, hf = divmod(c, halves)
        out[b, hf * ntok:(hf + 1) * ntok] = res.results[c]["out"]
    return out, res


def kernel(**inputs):
    out, _ = run_model(inputs, L=2, TW=1024, final=True)
    return out
```
